# Optimizing a Trainium2 kernel written in Bass

```python
import jax, jax.numpy as jnp
from jax import lax
import numpy as np

D_MODEL = 2048
BATCH = 4
SEQ = 2048
DEPTH = 1
DEC_BATCH = 128
DEC_SEQ = 4
PAST_LEN = 16384
PAGE_SIZE = 128

GLA_HEADS = 4
GLA_DK_TOTAL = D_MODEL // 2
GLA_DV_TOTAL = D_MODEL
GLA_DK = GLA_DK_TOTAL // GLA_HEADS
GLA_DV = GLA_DV_TOTAL // GLA_HEADS
GLA_GATE_RANK = 16
GLA_GATE_TAU = 16.0
GLA_CHUNK = 16
CONV_CHANNELS = D_MODEL
CONV_WIDTH = 31
D_FF = 4 * D_MODEL
NORM_EPS = 1e-6

IN_SPLIT = (GLA_DK_TOTAL, GLA_DK_TOTAL, GLA_DV_TOTAL, GLA_DV_TOTAL, GLA_GATE_RANK,
            CONV_CHANNELS, CONV_CHANNELS, D_MODEL, D_MODEL)
D_IN = sum(IN_SPLIT)

kernel_name = "gla_conformer_conv_gated_hybrid_step"


def rmsnorm(x, g):
    xf = x.astype(jnp.float32)
    y = xf * lax.rsqrt(jnp.mean(xf * xf, axis=-1, keepdims=True) + NORM_EPS)
    return (y * g.astype(jnp.float32)).astype(x.dtype)


def layernorm(x, g, b):
    xf = x.astype(jnp.float32)
    mu = jnp.mean(xf, axis=-1, keepdims=True)
    xc = xf - mu
    y = xc * lax.rsqrt(jnp.mean(xc * xc, axis=-1, keepdims=True) + NORM_EPS)
    return (y * g.astype(jnp.float32) + b.astype(jnp.float32)).astype(x.dtype)


def gla_chunked(q, k, v, log_a, s0):
    B, T = q.shape[0], q.shape[1]
    n = -(-T // GLA_CHUNK)
    pad = n * GLA_CHUNK - T
    padw = ((0, 0), (0, pad), (0, 0), (0, 0))
    f32 = jnp.float32

    def to_chunks(t):
        t = jnp.pad(t.astype(f32), padw)
        return t.reshape(B, n, GLA_CHUNK, GLA_HEADS, t.shape[-1]).transpose(1, 0, 3, 2, 4)

    qc, kc, vc, ac = to_chunks(q), to_chunks(k), to_chunks(v), to_chunks(log_a)
    b = jnp.cumsum(ac, axis=-2)
    b_last = b[..., -1, :]
    q_dec = qc * jnp.exp(b)
    k_inv = kc * jnp.exp(-b)
    k_tail = kc * jnp.exp(b_last[..., None, :] - b)
    causal = jnp.tril(jnp.ones((GLA_CHUNK, GLA_CHUNK), dtype=bool))
    attn = jnp.where(causal, jnp.einsum('nbhid,nbhjd->nbhij', q_dec, k_inv), 0.0)
    o_intra = jnp.einsum('nbhij,nbhjv->nbhiv', attn, vc)

    def step(s, xs):
        qd, kt, vv, bl = xs
        o = jnp.einsum('bhid,bhdv->bhiv', qd, s)
        s = jnp.exp(bl)[..., None] * s + jnp.einsum('bhjd,bhjv->bhdv', kt, vv)
        return s, o

    s_final, o_inter = lax.scan(step, s0.astype(f32), (q_dec, k_tail, vc, b_last))
    o = (o_intra + o_inter).transpose(1, 0, 3, 2, 4).reshape(B, n * GLA_CHUNK, GLA_HEADS, GLA_DV)
    return o[:, :T], s_final


def trunk_layer(x, s_gla, s_conv, w_in, w_a2, b_a, g_gla_norm, w_o_gla, w_dw, b_dw,
                g_ln, b_ln, w_pw2, w_out, g_norm1, g_norm2, w_ff1, w_ff2):
    B, T, _ = x.shape
    h = rmsnorm(x, g_norm1)
    z = h @ w_in
    split_idx = np.cumsum(IN_SPLIT)[:-1].tolist()
    q, k, v, g, a_lr, glu_a, glu_b, gate_a, gate_b = jnp.split(z, split_idx, axis=-1)

    log_a = jax.nn.log_sigmoid((a_lr @ w_a2 + b_a).astype(jnp.float32)) / GLA_GATE_TAU
    q = q.reshape(B, T, GLA_HEADS, GLA_DK) * (GLA_DK ** -0.5)
    k = k.reshape(B, T, GLA_HEADS, GLA_DK)
    v = v.reshape(B, T, GLA_HEADS, GLA_DV)
    log_a = log_a.reshape(B, T, GLA_HEADS, GLA_DK)
    o, s_gla_new = gla_chunked(q, k, v, log_a, s_gla)
    o = rmsnorm(o, g_gla_norm).astype(x.dtype)
    o = o.reshape(B, T, GLA_DV_TOTAL) * jax.nn.silu(g)
    y_a = o @ w_o_gla

    u = glu_a * jax.nn.sigmoid(glu_b)
    u_ext = jnp.concatenate([s_conv.astype(u.dtype), u], axis=1)
    c = lax.conv_general_dilated(u_ext, w_dw[:, None, :].astype(u.dtype), window_strides=(1,),
                                 padding='VALID', dimension_numbers=('NWC', 'WIO', 'NWC'),
                                 feature_group_count=CONV_CHANNELS) + b_dw
    c = layernorm(c, g_ln, b_ln)
    y_b = jax.nn.silu(c) @ w_pw2
    s_conv_new = u_ext[:, -(CONV_WIDTH - 1):]

    mixed = jax.nn.sigmoid(gate_a) * y_a + jax.nn.sigmoid(gate_b) * y_b
    x = x + mixed @ w_out

    hf = rmsnorm(x, g_norm2)
    x = x + jnp.square(jax.nn.relu(hf @ w_ff1)) @ w_ff2
    return x, s_gla_new.astype(s_gla.dtype), s_conv_new.astype(s_conv.dtype)


def setup_inputs(seed: int = 0) -> dict:
    key = jax.random.key(seed)
    ks = jax.random.split(key, 24)
    nrm = jax.random.normal
    f32 = jnp.float32
    L = DEPTH
    return {
        "x_prompt": nrm(ks[0], (BATCH, SEQ, D_MODEL), f32),
        "x_sample": nrm(ks[1], (DEC_BATCH, DEC_SEQ, D_MODEL), f32),
        "state_gla": 2.0 * nrm(ks[2], (L, DEC_BATCH, GLA_HEADS, GLA_DK, GLA_DV), f32),
        "state_conv": 0.5 * nrm(ks[3], (L, DEC_BATCH, CONV_WIDTH - 1, CONV_CHANNELS), f32),
        "w_in": nrm(ks[4], (L, D_MODEL, D_IN), f32) * D_MODEL ** -0.5,
        "w_a2": nrm(ks[5], (L, GLA_GATE_RANK, GLA_DK_TOTAL), f32) * GLA_GATE_RANK ** -0.5,
        "b_a": 0.1 * nrm(ks[6], (L, GLA_DK_TOTAL), f32),
        "g_gla_norm": 1.0 + 0.1 * nrm(ks[7], (L, GLA_DV), f32),
        "w_o_gla": nrm(ks[8], (L, GLA_DV_TOTAL, D_MODEL), f32) * GLA_DV_TOTAL ** -0.5,
        "w_dw": nrm(ks[9], (L, CONV_WIDTH, CONV_CHANNELS), f32) * CONV_WIDTH ** -0.5,
        "b_dw": 0.02 * nrm(ks[10], (L, CONV_CHANNELS), f32),
        "g_ln": 1.0 + 0.1 * nrm(ks[11], (L, CONV_CHANNELS), f32),
        "b_ln": 0.02 * nrm(ks[12], (L, CONV_CHANNELS), f32),
        "w_pw2": nrm(ks[13], (L, CONV_CHANNELS, D_MODEL), f32) * CONV_CHANNELS ** -0.5,
        "w_out": nrm(ks[14], (L, D_MODEL, D_MODEL), f32) * D_MODEL ** -0.5,
        "g_norm1": 1.0 + 0.1 * nrm(ks[15], (L, D_MODEL), f32),
        "g_norm2": 1.0 + 0.1 * nrm(ks[16], (L, D_MODEL), f32),
        "w_ff1": nrm(ks[17], (L, D_MODEL, D_FF), f32) * D_MODEL ** -0.5,
        "w_ff2": nrm(ks[18], (L, D_FF, D_MODEL), f32) * D_FF ** -0.5,
        "g_final": 1.0 + 0.1 * nrm(ks[19], (D_MODEL,), f32),
    }


def reference(x_prompt, x_sample, state_gla, state_conv, w_in, w_a2, b_a, g_gla_norm, w_o_gla,
              w_dw, b_dw, g_ln, b_ln, w_pw2, w_out, g_norm1, g_norm2, w_ff1, w_ff2, g_final):
    hp, hs = x_prompt, x_sample
    gla_p, conv_p, gla_s, conv_s = [], [], [], []
    for l in range(DEPTH):
        w = (w_in[l], w_a2[l], b_a[l], g_gla_norm[l], w_o_gla[l], w_dw[l], b_dw[l],
             g_ln[l], b_ln[l], w_pw2[l], w_out[l], g_norm1[l], g_norm2[l], w_ff1[l], w_ff2[l])
        zg = jnp.zeros((hp.shape[0], GLA_HEADS, GLA_DK, GLA_DV), state_gla.dtype)
        zc = jnp.zeros((hp.shape[0], CONV_WIDTH - 1, CONV_CHANNELS), state_conv.dtype)
        hp, sgp, scp = trunk_layer(hp, zg, zc, *w)
        hs, sgs, scs = trunk_layer(hs, state_gla[l], state_conv[l], *w)
        gla_p.append(sgp); conv_p.append(scp); gla_s.append(sgs); conv_s.append(scs)
    y_prompt = rmsnorm(hp, g_final)
    y_sample = rmsnorm(hs, g_final)
    state_gla_prompt = jnp.stack(gla_p)
    state_conv_prompt = jnp.stack(conv_p)
    state_gla_sample = jnp.stack(gla_s)
    state_conv_sample = jnp.stack(conv_s)
    return (y_prompt, y_sample, state_gla_prompt, state_conv_prompt, state_gla_sample, state_conv_sample)
```

```python
import contextlib
import types
import numpy as np
import concourse.bass as bass
import concourse.mybir as mybir
from concourse.bass_utils import run_bass_kernel_spmd

F32 = mybir.dt.float32
BF16 = mybir.dt.bfloat16
AF = mybir.ActivationFunctionType
ALU = mybir.AluOpType

DEBUG = False
D = 2048
DIN = 14352
NT = 1088
C_Q, C_K, C_V, C_G, C_ALR, C_GLUA, C_GLUB, C_GA, C_GB = 0, 1024, 2048, 4096, 6144, 6160, 8208, 10256, 12304
EPS = 1e-6
K_ID, K_TRI, K_TRIU, K_MASK, K_TRIS, K_TRIUS, K_MASKS, K_RM, K_SEL, K_LAST, K_ONESM = (
    0, 128, 256, 384, 512, 576, 640, 704, 720, 736, 738)
CW = 866


def _freeze(fn, depth=0):
    if not isinstance(fn, types.FunctionType) or fn.__closure__ is None or depth > 3:
        return fn
    cells = []
    for c in fn.__closure__:
        try:
            v = c.cell_contents
        except ValueError:
            cells.append(c)
            continue
        if isinstance(v, types.FunctionType):
            v = _freeze(v, depth + 1)
        cells.append(types.CellType(v))
    g = types.FunctionType(fn.__code__, fn.__globals__, fn.__name__, fn.__defaults__, tuple(cells))
    g.__kwdefaults__ = fn.__kwdefaults__
    return g


class Res:
    __slots__ = ("w", "r")

    def __init__(self):
        self.w = None
        self.r = {}


class Prog:
    ENGS = ("pe", "dve", "act", "pool", "sp")

    def __init__(self):
        self.ops = {e: [] for e in self.ENGS}
        self.cnt = {}
        self.known = {e: {} for e in self.ENGS}
        self.semkeys = []
        self.dma_i = {}

    DMA_RING = {"w": 8, "in": 8, "out": 8, "st_in": 4, "st_out": 4}

    def _sem(self, key):
        if key not in self.cnt:
            self.cnt[key] = 0
            self.semkeys.append(key)
        return key

    def op(self, eng, fn, reads=(), writes=(), dma=None):
        fn = _freeze(fn)
        waits = {}

        def need(tok):
            if tok is None:
                return
            k, v = tok
            if self.known[eng].get(k, 0) >= v:
                return
            if waits.get(k, 0) < v:
                waits[k] = v

        for r in reads:
            need(r.w)
        for w in writes:
            need(w.w)
            for k, v in w.r.items():
                need((k, v))
        if dma is None:
            key = self._sem("e_" + eng)
            inc = 1
        else:
            i = self.dma_i.get(dma, 0)
            self.dma_i[dma] = i + 1
            key = self._sem("dma_%s_%d" % (dma, i % self.DMA_RING.get(dma, 8)))
            inc = 16
            need((key, self.cnt[key]))
        for k, v in waits.items():
            self.known[eng][k] = v
        self.cnt[key] += inc
        tok = (key, self.cnt[key])
        for r in reads:
            if r.r.get(key, 0) < tok[1]:
                r.r[key] = tok[1]
        for w in writes:
            w.w = tok
            w.r = {}
        self.ops[eng].append((list(waits.items()), fn, key, inc))
        return tok

    def fence(self, engs, resources):
        for e in engs:
            waits = {}
            for r in resources:
                toks = list(r.r.items()) + ([r.w] if r.w is not None else [])
                for k, v in toks:
                    if self.known[e].get(k, 0) < v and waits.get(k, 0) < v:
                        waits[k] = v
            for k, v in waits.items():
                self.known[e][k] = v
            if waits:
                self.ops[e].append((list(waits.items()), None, None, 0))

    def barrier(self, engs=("dve", "act", "sp")):
        for e in engs:
            waits = []
            for k in self.semkeys:
                v = self.cnt[k]
                if v > 0 and self.known[e].get(k, 0) < v:
                    waits.append((k, v))
                    self.known[e][k] = v
            if waits:
                self.ops[e].append((waits, None, None, 0))

    def emit(self, nc, final_eng="sp"):
        with contextlib.ExitStack() as st:
            sems = {k: st.enter_context(nc.semaphore(k)) for k in self.semkeys}
            block = st.enter_context(nc.Block())
            engmap = {"pe": block.tensor, "dve": block.vector, "act": block.scalar,
                      "pool": block.gpsimd, "sp": block.sync}
            for e in self.ENGS:
                ops = self.ops[e]
                final = (e == final_eng)

                def body(engine, ops=ops, final=final):
                    for waits, fn, key, inc in ops:
                        for k, v in waits:
                            engine.wait_ge(sems[k], v)
                        if fn is not None:
                            fn(engine).then_inc(sems[key], inc)
                    if final:
                        for k in self.semkeys:
                            if self.cnt[k] > 0:
                                engine.wait_ge(sems[k], self.cnt[k])

                engmap[e](body)


class Arena:
    def __init__(self, nc, st, name, nbytes):
        self.t = st.enter_context(nc.sbuf_tensor(name, [128, nbytes // 2], BF16))
        self.nbytes = nbytes
        self.off = 0

    def reset(self):
        self.off = 0

    def alloc(self, shape, dt):
        n = 1
        for s in shape:
            n *= s
        nb = n * (4 if dt == F32 else 2)
        nb = (nb + 31) // 32 * 32
        assert self.off + nb <= self.nbytes, (self.off, nb, self.nbytes)
        v = self.t[:, self.off // 2:(self.off + nb) // 2]
        if dt == F32:
            v = v.bitcast(F32)
        v = v[:, 0:n]
        if len(shape) == 2:
            v = v.rearrange("p (a b) -> p a b", a=shape[0])
        elif len(shape) == 3:
            v = v.rearrange("p (a b c) -> p a b c", a=shape[0], b=shape[1])
        self.off += nb
        return v


class Ring:
    def __init__(self, items):
        self.items = [(a, Res()) for a in items]
        self.i = 0
        self.held = set()

    def next(self, exclude=None):
        for _ in range(2 * len(self.items)):
            it = self.items[self.i % len(self.items)]
            self.i += 1
            if (exclude is not None and it[1] is exclude) or (id(it[1]) in self.held):
                continue
            return it
        raise RuntimeError("ring exhausted")

    def hold(self, res):
        self.held.add(id(res))

    def release(self, res):
        self.held.discard(id(res))


def build_program():
    nc = bass.Bass("TRN2", target_bir_lowering=False)

    def din(name, shape):
        return nc.dram_tensor(name, list(shape), F32, kind="ExternalInput").ap()

    def dout(name, shape):
        return nc.dram_tensor(name, list(shape), F32, kind="ExternalOutput").ap()

    x_main = din("x_main", [1024, D])
    x_pre = din("x_pre", [1024, D])
    x_smp = din("x_smp", [64, D])
    sg_in = din("sg_in", [16, 4, 256, 512])
    sc_in = din("sc_in", [16, 30, D])
    w_in = din("w_in", [D, DIN])
    w_a2 = din("w_a2", [16, 1024])
    b_a = din("b_a", [1, 1024])
    ggn = din("ggn", [1, 512])
    w_o_gla = din("w_o_gla", [D, D])
    w_dw = din("w_dw", [31, D])
    vecs = din("vecs", [48, 128])
    w_pw2 = din("w_pw2", [D, D])
    w_out = din("w_out", [D, D])
    g_n1 = din("g_n1", [1, D])
    g_n2 = din("g_n2", [1, D])
    g_fin = din("g_fin", [1, D])
    w_ff1 = din("w_ff1", [D, 4 * D])
    w_ff2 = din("w_ff2", [4 * D, D])
    consts = din("consts", [128, CW])

    y_main = dout("y_main", [1024, D])
    y_smp = dout("y_smp", [64, D])
    sg_p = dout("sg_p", [4, 256, 512])
    sc_p = dout("sc_p", [30, D])
    sg_s = dout("sg_s", [16, 4, 256, 512])
    sc_s = dout("sc_s", [16, 30, D])

    w_in_v = w_in.rearrange("(kc p) n -> p kc n", p=128)
    w_o_v = w_o_gla.rearrange("(kc p) n -> p kc n", p=128)
    w_pw_v = w_pw2.rearrange("(kc p) n -> p kc n", p=128)
    w_out_v = w_out.rearrange("(kc p) n -> p kc n", p=128)
    w_ff1_v = w_ff1.rearrange("(kc p) n -> p kc n", p=128)
    w_ff2_v = w_ff2.rearrange("(kc p) n -> p kc n", p=128)

    dbg = {}
    if DEBUG:
        dbg["oT"] = nc.dram_tensor("dbg_oT", [128, 16, NT], BF16, kind="ExternalOutput").ap()
        dbg["cb"] = nc.dram_tensor("dbg_cb", [128, 16, NT], BF16, kind="ExternalOutput").ap()
        dbg["mixT"] = nc.dram_tensor("dbg_mixT", [128, 16, NT], BF16, kind="ExternalOutput").ap()
        dbg["hT"] = nc.dram_tensor("dbg_hT", [128, 16, NT], BF16, kind="ExternalOutput").ap()
        dbg["x1"] = nc.dram_tensor("dbg_x1", [9, 128, D], F32, kind="ExternalOutput").ap()
        dbg["O"] = nc.dram_tensor("dbg_O", [64, 512], F32, kind="ExternalOutput").ap()
        dbg["Qm"] = nc.dram_tensor("dbg_Qm", [128, 2, 16, 64], BF16, kind="ExternalOutput").ap()
        dbg["qkT"] = nc.dram_tensor("dbg_qkT", [128, 4, 128], BF16, kind="ExternalOutput").ap()
        dbg["AT"] = nc.dram_tensor("dbg_AT", [128, 128], BF16, kind="ExternalOutput").ap()
        dbg["cb0"] = nc.dram_tensor("dbg_cb0", [128, 16, NT], BF16, kind="ExternalOutput").ap()
        dbg["stat"] = nc.dram_tensor("dbg_stat", [3, 128, 512], F32, kind="ExternalOutput").ap()
        dbg["wdwT"] = nc.dram_tensor("dbg_wdwT", [128, 16, 31], F32, kind="ExternalOutput").ap()
        dbg["vecT"] = nc.dram_tensor("dbg_vecT", [128, 48], F32, kind="ExternalOutput").ap()
    P = Prog()
    st = contextlib.ExitStack()
    with st:
        AW = Arena(nc, st, "AW", 2 * 16384)
        R1 = Arena(nc, st, "R1", 34816)
        R2 = Arena(nc, st, "R2", 34816)
        R3 = Arena(nc, st, "R3", 34816)
        R4 = Arena(nc, st, "R4", 34816)
        R5 = Arena(nc, st, "R5", 18432)
        AC = Arena(nc, st, "AC", 14336)
        banks = Ring([st.enter_context(nc.psum_tensor(f"bank{i}", [128, 512], F32)) for i in range(5)])
        bkOS = st.enter_context(nc.psum_tensor("bankOS", [128, 512], F32))
        r_bkOS = Res()
        tbanks = Ring([st.enter_context(nc.psum_tensor(f"tbank{i}", [128, 1024], BF16)) for i in range(2)])
        wslots = Ring([AW.alloc((16, 512), BF16) for _ in range(2)])

        cst = AC.alloc((CW,), F32)
        ident_b = AC.alloc((128,), BF16)
        onesm_b = AC.alloc((128,), BF16)
        w_a2b = AC.alloc((1024,), F32)
        ggn_b = AC.alloc((512,), F32)
        walr = AC.alloc((16, 16), BF16)
        eps_t = AC.alloc((1,), F32)
        one_t = AC.alloc((1,), F32)
        vecT = AC.alloc((48,), F32)
        wdwT = AC.alloc((16, 31), F32)
        r_cst = Res()

        ident_f = cst[:, K_ID:K_ID + 128]

        def evac_copy(eng, out, in_, reads, writes):
            if eng == "act":
                P.op("act", lambda e: e.activation(out=out, in_=in_, func=AF.Copy), reads=reads, writes=writes)
            else:
                P.op(eng, lambda e: e.tensor_copy(out=out, in_=in_), reads=reads, writes=writes)

        def mm_group(out, pairs, reads, writes):
            def f(e):
                n = len(pairs)
                ins = None
                for i, (l, r) in enumerate(pairs):
                    ins = e.matmul(out, lhsT=l, rhs=r, start=(i == 0), stop=(i == n - 1))
                return ins
            P.op("pe", f, reads=reads, writes=writes)

        def load_w(parts):
            slot, res = wslots.next()
            for (src, c0, n) in parts:
                P.op("pool", lambda e, src=src, c0=c0, n=n: e.dma_start(out=slot[:, :, c0:c0 + n], in_=src),
                     writes=[res], dma="w")
            return slot, res

        P.op("sp", lambda e: e.dma_start(out=cst, in_=consts), writes=[r_cst], dma="in")
        P.op("sp", lambda e: e.dma_start(out=w_a2b[0:16, :], in_=w_a2), writes=[r_cst], dma="in")
        P.op("sp", lambda e: e.dma_start(out=w_a2b[16:17, :], in_=b_a), writes=[r_cst], dma="in")
        P.op("sp", lambda e: e.dma_start(out=ggn_b, in_=ggn[0:1, :].broadcast_to([128, 512])), writes=[r_cst], dma="in")
        P.op("pool", lambda e: e.dma_start(out=walr, in_=w_in_v[:, :, C_ALR:C_ALR + 16]), writes=[r_cst], dma="w")
        P.op("dve", lambda e: e.memset(eps_t, EPS), writes=[r_cst])
        P.op("dve", lambda e: e.memset(one_t, 1.0), writes=[r_cst])
        P.op("dve", lambda e: e.tensor_copy(out=ident_b, in_=ident_f), reads=[r_cst], writes=[r_cst])
        P.op("dve", lambda e: e.tensor_copy(out=onesm_b, in_=cst[:, K_ONESM:K_ONESM + 128]), reads=[r_cst], writes=[r_cst])
        tmpv = R5.alloc((128,), F32)
        r_tmpv = Res()
        P.op("dve", lambda e: e.memset(tmpv, 0.0), writes=[r_tmpv])
        P.op("sp", lambda e: e.dma_start(out=tmpv[0:48, :], in_=vecs), writes=[r_tmpv], dma="in")
        bk, rb_ = banks.next()
        P.op("pe", lambda e: e.transpose(out=bk[:, 0:128], in_=tmpv, identity=ident_f),
             reads=[r_tmpv, r_cst], writes=[rb_])
        P.op("dve", lambda e: e.tensor_copy(out=vecT, in_=bk[:, 0:48]), reads=[rb_], writes=[r_cst])
        tmpw = R1.alloc((D,), F32)
        r_tmpw = Res()
        P.op("dve", lambda e: e.memset(tmpw, 0.0), writes=[r_tmpw])
        P.op("sp", lambda e: e.dma_start(out=tmpw[0:31, :], in_=w_dw), writes=[r_tmpw], dma="in")
        for g4 in range(4):
            bk, rb_ = banks.next()

            def f_wdw(e, bk=bk, g4=g4):
                ins = None
                for j in range(4):
                    cc = g4 * 4 + j
                    ins = e.transpose(out=bk[:, j * 128:(j + 1) * 128], in_=tmpw[:, cc * 128:(cc + 1) * 128], identity=ident_f)
                return ins
            P.op("pe", f_wdw, reads=[r_tmpw, r_cst], writes=[rb_])
            P.op("dve", lambda e, bk=bk, g4=g4: e.tensor_copy(out=wdwT[:, g4 * 4:(g4 + 1) * 4, :],
                                                                in_=bk[:, 0:512].rearrange("p (a b) -> p a b", a=4)[:, :, 0:31]),
                 reads=[rb_], writes=[r_cst])
        P.barrier()
        R5.reset(); R1.reset()

        def rms_to_T(x_src, T, gb, r_gb, dstT, r_dst, col0, xring, hring, junk, r_junk, small):
            xt, r_xt = xring.next()
            hb, r_hb = hring.next()
            P.op("sp", lambda e: e.dma_start(out=xt[0:T, :], in_=x_src), writes=[r_xt], dma="in")
            rms_sbuf(xt, r_xt, T, gb, r_gb, hb, r_hb, junk, r_junk, small)
            transpose_to_T(hb, r_hb, T, dstT, r_dst, col0)

        def rms_sbuf(xt, r_xt, T, gb, r_gb, hb, r_hb, junk, r_junk, small):
            ss, r_ss = small.next()
            P.op("act", lambda e: e.activation(out=junk[0:T, :], in_=xt[0:T, :], func=AF.Square, accum_out=ss[0:T, 0:1]),
                 reads=[r_xt], writes=[r_junk, r_ss])
            P.op("act", lambda e: e.activation(out=ss[0:T, 1:2], in_=ss[0:T, 0:1], func=AF.Ln, scale=1.0 / D,
                                               bias=eps_t[0:T, :]), reads=[r_ss, r_cst], writes=[r_ss])
            P.op("act", lambda e: e.activation(out=ss[0:T, 2:3], in_=ss[0:T, 1:2], func=AF.Exp, scale=-0.5),
                 reads=[r_ss], writes=[r_ss])
            P.op("dve", lambda e: e.scalar_tensor_tensor(out=hb[0:T, :], in0=xt[0:T, :], scalar=ss[0:T, 2:3],
                                                         in1=gb[0:T, :], op0=ALU.mult, op1=ALU.mult),
                 reads=[r_xt, r_ss, r_gb], writes=[r_hb])

        def rms_pipeline(blocks, gb, r_gb, dstT, r_dst, xring, hring, junk, r_junk, small):
            st = {}

            def s1(b):
                T, col0, src, xs = blocks[b]
                if xs is None:
                    xt, r_xt = xring.next()
                    P.op("sp", lambda e: e.dma_start(out=xt[0:T, :], in_=src), writes=[r_xt], dma="in")
                else:
                    xt, r_xt = xs
                ss, r_ss = small.next()
                P.op("act", lambda e: e.activation(out=junk[0:T, :], in_=xt[0:T, :], func=AF.Square, accum_out=ss[0:T, 0:1]),
                     reads=[r_xt], writes=[r_junk, r_ss])
                P.op("act", lambda e: e.activation(out=ss[0:T, 1:2], in_=ss[0:T, 0:1], func=AF.Ln, scale=1.0 / D,
                                                   bias=eps_t[0:T, :]), reads=[r_ss, r_cst], writes=[r_ss])
                P.op("act", lambda e: e.activation(out=ss[0:T, 2:3], in_=ss[0:T, 1:2], func=AF.Exp, scale=-0.5),
                     reads=[r_ss], writes=[r_ss])
                st[b] = [xt, r_xt, ss, r_ss]

            def s2(b):
                T, col0, src, xs = blocks[b]
                xt, r_xt, ss, r_ss = st[b]
                hb, r_hb = hring.next()
                P.op("dve", lambda e: e.scalar_tensor_tensor(out=hb[0:T, :], in0=xt[0:T, :], scalar=ss[0:T, 2:3],
                                                             in1=gb[0:T, :], op0=ALU.mult, op1=ALU.mult),
                     reads=[r_xt, r_ss, r_gb], writes=[r_hb])
                st[b] = [hb, r_hb]

            def s3(b):
                T, col0, src, xs = blocks[b]
                hb, r_hb = st[b]
                transpose_to_T(hb, r_hb, T, dstT, r_dst, col0)

            n = len(blocks)
            s1(0)
            if n > 1:
                s1(1)
            s2(0)
            for b in range(n):
                if b + 2 < n:
                    s1(b + 2)
                if b + 1 < n:
                    s2(b + 1)
                s3(b)

        tcount = [0]

        def transpose_to_T(hb, r_hb, T, dstT, r_dst, col0, nchunks=16, kc0=0):
            for g in range(0, nchunks, 4):
                ng = min(4, nchunks - g)
                tb, r_tb = tbanks.next()

                def f(e, g=g, ng=ng, tb=tb):
                    ins = None
                    for j in range(ng):
                        ins = e.transpose(out=tb[:, j * 128:j * 128 + T], in_=hb[0:T, (g + j) * 128:(g + j + 1) * 128],
                                          identity=ident_b[0:T, 0:T])
                    return ins
                P.op("pe", f, reads=[r_hb, r_cst], writes=[r_tb])
                eng = "dve" if (tcount[0] % 2 == 0) else "act"
                tcount[0] += 1
                evac_copy(eng, dstT[:, kc0 + g:kc0 + g + ng, col0:col0 + T],
                          tb[:, 0:ng * 128].rearrange("p (a b) -> p a b", a=ng)[:, :, 0:T], [r_tb], [r_dst])

        def run_gens(gens):
            gens = list(gens)
            while gens:
                for g in list(gens):
                    try:
                        next(g)
                    except StopIteration:
                        gens.remove(g)

        def gla_gates_g(alr, r_alr, col0, T, h, sample, G, need_b, out, dec_tile=None):
            tri = cst[:, (K_TRIS if sample else K_TRI):][:, 0:T]
            triu = cst[:, (K_TRIUS if sample else K_TRIU):][:, 0:T]
            bk, r_bk = banks.next()
            mm_group(bk[0:T, 0:256], [(alr[0:17, col0:col0 + T], w_a2b[0:17, h * 256:(h + 1) * 256])],
                     [r_alr, r_cst], [r_bk])
            e1, r_e1 = G["e1"].next()
            P.op("act", lambda e: e.activation(out=e1[0:T, :], in_=bk[0:T, 0:256], func=AF.Exp, scale=-1.0),
                 reads=[r_bk], writes=[r_e1])
            nla, r_nla = G["nla"].next()
            P.op("act", lambda e: e.activation(out=nla[0:T, :], in_=e1[0:T, :], func=AF.Ln, bias=one_t[0:T, :]),
                 reads=[r_e1, r_cst], writes=[r_nla])
            yield
            bk2, r_bk2 = banks.next()
            mm_group(bk2[0:T, 0:256], [(triu[0:T, :], nla[0:T, :])], [r_nla, r_cst], [r_bk2])
            erb, r_erb = G["erb"].next()
            P.op("act", lambda e: e.activation(out=erb[0:T, :], in_=bk2[0:T, 0:256], func=AF.Exp),
                 reads=[r_bk2], writes=[r_erb])
            out["erb"] = (erb, r_erb)
            if need_b:
                bk3, r_bk3 = banks.next()
                mm_group(bk3[0:T, 0:256], [(tri[0:T, :], nla[0:T, :])], [r_nla, r_cst], [r_bk3])
                eb, r_eb = G["eb"].next()
                enb, r_enb = G["enb"].next()
                P.op("act", lambda e: e.activation(out=eb[0:T, :], in_=bk3[0:T, 0:256], func=AF.Exp),
                     reads=[r_bk3], writes=[r_eb])
                P.op("act", lambda e: e.activation(out=enb[0:T, :], in_=bk3[0:T, 0:256], func=AF.Exp, scale=-1.0),
                     reads=[r_bk3], writes=[r_enb])
                out["eb"] = (eb, r_eb)
                out["enb"] = (enb, r_enb)
            bk4, r_bk4 = banks.next()
            nd = 16 if sample else 2
            kcol = K_SEL if sample else K_LAST

            def fd(e):
                ins = None
                for dc in range(2):
                    ins = e.matmul(bk4[:, dc * nd:dc * nd + nd], lhsT=nla[0:T, dc * 128:(dc + 1) * 128],
                                   rhs=cst[0:T, kcol:kcol + nd], start=True, stop=True)
                return ins
            P.op("pe", fd, reads=[r_nla, r_cst], writes=[r_bk4])
            dec, r_dec = dec_tile if dec_tile is not None else G["dec"].next()
            P.op("act", lambda e: e.activation(out=dec[:, 0:2 * nd], in_=bk4[:, 0:2 * nd], func=AF.Exp),
                 reads=[r_bk4], writes=[r_dec])
            out["dec"] = (dec, r_dec)
            yield

        def state_update_g(kt, r_kt, T, v_ap, r_v, S, r_S, dec, r_dec, deccol):
            for dc in range(2):
                bk, r_bk = banks.next()
                mm_group(bk[:, :], [(kt[0:T, dc * 128:(dc + 1) * 128], v_ap)], [r_kt, r_v], [r_bk])
                c = deccol(dc)
                P.op("dve", lambda e, dc=dc, bk=bk, c=c: e.scalar_tensor_tensor(
                    out=S[:, dc, :], in0=S[:, dc, :], scalar=dec[:, c:c + 1], in1=bk[:, :],
                    op0=ALU.mult, op1=ALU.add), reads=[r_bk, r_dec, r_S], writes=[r_S])
            yield

        S_all = R5.alloc((4, 2, 512), F32)
        r_S = [Res() for _ in range(4)]
        hp_tail = AC.alloc((16, 32), BF16)
        r_hpt = Res()
        P.op("dve", lambda e: e.memset(S_all, 0.0), writes=r_S)

        hpT = R4.alloc((16, 1024), BF16)
        r_hpT = Res()
        gb1 = R1.alloc((D,), F32)
        r_gb1 = Res()
        P.op("sp", lambda e: e.dma_start(out=gb1, in_=g_n1[0:1, :].broadcast_to([128, D])), writes=[r_gb1], dma="in")
        xring = Ring([R1.alloc((D,), F32) for _ in range(2)])
        hring = Ring([R1.alloc((D,), BF16) for _ in range(2)])
        junk = R3.alloc((D,), BF16)
        r_junk = Res()
        small = Ring([R3.alloc((4,), F32) for _ in range(4)])
        rms_pipeline([(128, tb * 128, x_pre[tb * 128:(tb + 1) * 128, :], None) for tb in range(8)],
                     gb1, r_gb1, hpT, r_hpT, xring, hring, junk, r_junk, small)
        P.op("dve", lambda e: e.tensor_copy(out=hp_tail, in_=hpT[:, :, 992:1024]), reads=[r_hpT], writes=[r_hpt])

        def make_alr(actT, r_actT, ntok, alr, r_alr):
            P.op("dve", lambda e: e.memset(alr[0:17, 0:ntok], 1.0), writes=[r_alr])
            c = 0
            while c < ntok:
                n = min(512, ntok - c)
                bk, r_bk = banks.next()
                mm_group(bk[0:16, 0:n], [(walr[:, kc, :], actT[:, kc, c:c + n]) for kc in range(16)],
                         [r_cst, r_actT], [r_bk])
                P.op("dve", lambda e, bk=bk, c=c, n=n: e.tensor_copy(out=alr[0:16, c:c + n], in_=bk[0:16, 0:n]),
                     reads=[r_bk], writes=[r_alr])
                c += n

        alrp = R2.alloc((1024,), F32)
        r_alrp = Res()
        make_alr(hpT, r_hpT, 1024, alrp, r_alrp)

        P.barrier()
        R1.reset()
        vp = R1.alloc((8, 2048), BF16)
        kp = R3.alloc((8, 1024), BF16)
        r_kp = [Res() for _ in range(4)]
        r_vp = [Res() for _ in range(4)]
        G = {k: Ring([R2.alloc((256,), F32) for _ in range(2)]) for k in ("e1", "nla", "erb", "eb", "enb")}
        G["dec"] = Ring([R2.alloc((32,), F32) for _ in range(2)])
        ktr = Ring([R2.alloc((256,), BF16) for _ in range(2)])
        pslots = {}

        def pproj(h, tb):
            if tb == 0:
                pslots[h] = (load_w([(w_in_v[:, :, C_K + h * 256:C_K + (h + 1) * 256], 0, 256)]),
                             load_w([(w_in_v[:, :, C_V + h * 512:C_V + (h + 1) * 512], 0, 512)]))
            (ks, r_ks), (vs, r_vs) = pslots[h]
            bk, r_bk = banks.next()
            mm_group(bk[:, 0:256], [(hpT[:, kc, tb * 128:(tb + 1) * 128], ks[:, kc, 0:256]) for kc in range(16)],
                     [r_hpT, r_ks], [r_bk])
            evac_copy("act" if tb % 2 else "dve", kp[:, tb, h * 256:(h + 1) * 256], bk[:, 0:256], [r_bk], [r_kp[h]])
            bk, r_bk = banks.next()
            mm_group(bk[:, :], [(hpT[:, kc, tb * 128:(tb + 1) * 128], vs[:, kc, :]) for kc in range(16)],
                     [r_hpT, r_vs], [r_bk])
            evac_copy("dve" if tb % 2 else "act", vp[:, tb, h * 512:(h + 1) * 512], bk[:, :], [r_bk], [r_vp[h]])

        def pA(h, tb, out):
            gg = {}
            yield from gla_gates_g(alrp, r_alrp, tb * 128, 128, h, False, G, False, gg)
            erb, r_erb = gg["erb"]
            kt, r_kt = ktr.next()
            P.op("dve", lambda e: e.tensor_tensor(out=kt, in0=kp[:, tb, h * 256:(h + 1) * 256], in1=erb, op=ALU.mult),
                 reads=[r_kp[h], r_erb], writes=[r_kt])
            out["v"] = (kt, r_kt, gg["dec"][0], gg["dec"][1])
            yield

        def pB(h, tb, ctx):
            kt, r_kt, dec, r_dec = ctx["v"]
            yield from state_update_g(kt, r_kt, 128, vp[:, tb, h * 512:(h + 1) * 512], r_vp[h], S_all[:, h], r_S[h], dec, r_dec,
                                      lambda dc: dc * 2)

        def pP(h, tb):
            yield
            pproj(h, tb)
            yield

        for tb in range(8):
            pproj(0, tb)
        for h in range(4):
            ctx = {}
            run_gens([pA(h, 0, ctx)])
            for tb in range(8):
                gens = []
                nxt = {}
                if tb + 1 < 8:
                    gens.append(pA(h, tb + 1, nxt))
                gens.append(pB(h, tb, ctx))
                if h + 1 < 4:
                    gens.append(pP(h + 1, tb))
                run_gens(gens)
                ctx = nxt
        P.barrier()
        R1.reset(); R2.reset(); R3.reset(); R4.reset()

        hT = R1.alloc((16, NT), BF16)
        r_hT = Res()
        oT = R2.alloc((16, NT), BF16)
        r_oT = Res()
        gb1 = R4.alloc((D,), F32)
        r_gb1 = Res()
        P.op("sp", lambda e: e.dma_start(out=gb1, in_=g_n1[0:1, :].broadcast_to([128, D])), writes=[r_gb1], dma="in")
        xring = Ring([R4.alloc((D,), F32) for _ in range(2)])
        hring = Ring([R4.alloc((D,), BF16) for _ in range(2)])
        alr = R3.alloc((NT,), F32)
        r_alr = Res()
        alr_end = R3.off
        junk = R3.alloc((D,), BF16)
        r_junk = Res()
        small = Ring([R3.alloc((4,), F32) for _ in range(4)])
        rms_pipeline([(128, tb * 128, x_main[tb * 128:(tb + 1) * 128, :], None) for tb in range(8)] + [(64, 1024, x_smp, None)],
                     gb1, r_gb1, hT, r_hT, xring, hring, junk, r_junk, small)
        make_alr(hT, r_hT, NT, alr, r_alr)
        P.barrier()
        R4.reset()
        R3.off = alr_end
        qk = R3.alloc((9, 512), BF16)
        vv = R3.alloc((9, 512), BF16)
        gs = R3.alloc((9, 512), BF16)
        r_qkb = [Res() for _ in range(9)]
        r_vvb = [Res() for _ in range(9)]
        r_gsb = [Res() for _ in range(9)]
        G = {k: Ring([R4.alloc((256,), F32) for _ in range(1)]) for k in ("e1", "nla", "erb", "eb", "enb")}
        G["dec"] = Ring([R3.alloc((32,), F32) for _ in range(2)])
        qdr = Ring([R4.alloc((256,), BF16) for _ in range(2)])
        kir = Ring([R4.alloc((256,), BF16) for _ in range(2)])
        ktr = Ring([R4.alloc((256,), BF16) for _ in range(2)])
        qkTr = Ring([R4.alloc((4, 128), BF16) for _ in range(2)])
        ATr = Ring([R4.alloc((128,), BF16) for _ in range(2)])
        onr = Ring([R4.alloc((512,), BF16) for _ in range(2)])
        junk2 = R3.alloc((512,), BF16)
        r_junk2 = Res()
        small = Ring([R3.alloc((4,), F32) for _ in range(4)])
        Sbf = Ring([R4.alloc((2, 512), BF16) for _ in range(2)])
        QmT = R4.alloc((2, 16, 64), BF16)
        r_QmT = Res()
        Ktmr = Ring([R4.alloc((256,), BF16) for _ in range(2)])
        S0r = Ring([R4.alloc((2, 512), F32) for _ in range(2)])
        S0br = Ring([R4.alloc((2, 512), BF16) for _ in range(2)])
        P.op("dve", lambda e: e.memset(QmT, 0.0), writes=[r_QmT])
        ktS = R5.alloc((256,), BF16); r_ktS = Res()
        qkTS = R3.alloc((4, 128), BF16); r_qkTS = Res()
        ATS = R3.alloc((128,), BF16); r_ATS = Res()
        vS = R5.alloc((512,), BF16); r_vS = Res()
        decS = R5.alloc((32,), F32); r_decS = Res()
        pm_slots = {}

        def proj_unit(hh, kind, tb):
            if (hh, kind) not in pm_slots:
                if kind == "qk":
                    pm_slots[(hh, kind)] = load_w([(w_in_v[:, :, C_Q + hh * 256:C_Q + (hh + 1) * 256], 0, 256),
                                                   (w_in_v[:, :, C_K + hh * 256:C_K + (hh + 1) * 256], 256, 256)])
                elif kind == "v":
                    pm_slots[(hh, kind)] = load_w([(w_in_v[:, :, C_V + hh * 512:C_V + (hh + 1) * 512], 0, 512)])
                else:
                    pm_slots[(hh, kind)] = load_w([(w_in_v[:, :, C_G + hh * 512:C_G + (hh + 1) * 512], 0, 512)])
            slot, r_sl = pm_slots[(hh, kind)]
            T = 128 if tb < 8 else 64
            bk, r_bk = banks.next()
            mm_group(bk[0:T, :], [(hT[:, kc, tb * 128:tb * 128 + T], slot[:, kc, :]) for kc in range(16)],
                     [r_hT, r_sl], [r_bk])
            if kind == "qk":
                evac_copy("act" if tb % 2 else "dve", qk[0:T, tb, :], bk[0:T, :], [r_bk], [r_qkb[tb]])
            elif kind == "v":
                evac_copy("dve" if tb % 2 else "act", vv[0:T, tb, :], bk[0:T, :], [r_bk], [r_vvb[tb]])
            else:
                P.op("act", lambda e: e.activation(out=gs[0:T, tb, :], in_=bk[0:T, :], func=AF.Silu),
                     reads=[r_bk], writes=[r_gsb[tb]])
                P.op("pool", lambda e: e.tensor_tensor(out=gs[0:T, tb, :], in0=gs[0:T, tb, :],
                                                       in1=ggn_b[0:T, :], op=ALU.mult),
                     reads=[r_gsb[tb], r_cst], writes=[r_gsb[tb]])

        def proj_gen(units):
            for (hh, kind, tb) in units:
                yield
                proj_unit(hh, kind, tb)
                yield

        mask = cst[:, K_MASK:K_MASK + 128]
        maskS = cst[:, K_MASKS:K_MASKS + 64]

        for h in range(4):
            if h == 0:
                for tb in range(9):
                    proj_unit(0, "qk", tb)
                for tb in range(9):
                    proj_unit(0, "v", tb)
                for tb in range(9):
                    proj_unit(0, "g", tb)
            S = S_all[:, h]
            sbf0, r_sbf0 = Sbf.next()
            evac_copy("act", sbf0, S, [r_S[h]], [r_sbf0])

            def stageA(tb, out):
                sample = (tb == 8)
                T = 64 if sample else 128
                gg = {}
                yield from gla_gates_g(alr, r_alr, tb * 128, T, h, sample, G, True, gg,
                                       dec_tile=(decS, r_decS) if sample else None)
                eb, r_eb = gg["eb"]
                enb, r_enb = gg["enb"]
                erb, r_erb = gg["erb"]
                qd, r_qd = qdr.next()
                ki, r_ki = kir.next()
                kt, r_kt = (ktS, r_ktS) if sample else ktr.next()
                P.op("dve", lambda e: e.scalar_tensor_tensor(
                    out=qd[0:T, :], in0=qk[0:T, tb, 0:256], scalar=0.0625, in1=eb[0:T, :], op0=ALU.mult, op1=ALU.mult),
                    reads=[r_qkb[tb], r_eb], writes=[r_qd])
                P.op("dve", lambda e: e.tensor_tensor(
                    out=ki[0:T, :], in0=qk[0:T, tb, 256:512], in1=enb[0:T, :], op=ALU.mult),
                    reads=[r_qkb[tb], r_enb], writes=[r_ki])
                P.op("dve", lambda e: e.tensor_tensor(
                    out=kt[0:T, :], in0=qk[0:T, tb, 256:512], in1=erb[0:T, :], op=ALU.mult),
                    reads=[r_qkb[tb], r_erb], writes=[r_kt])
                yield
                tbk, r_tbk = tbanks.next()

                def ftr(e):
                    ins = None
                    for j in range(4):
                        src = qd if j < 2 else ki
                        dc = j % 2
                        ins = e.transpose(out=tbk[:, j * 128:j * 128 + T], in_=src[0:T, dc * 128:(dc + 1) * 128],
                                          identity=ident_b[0:T, 0:T])
                    return ins
                P.op("pe", ftr, reads=[r_qd, r_ki, r_cst], writes=[r_tbk])
                qkT, r_qkT = (qkTS, r_qkTS) if sample else qkTr.next()
                evac_copy("dve", qkT[:, :, 0:T], tbk[:, 0:512].rearrange("p (a b) -> p a b", a=4)[:, :, 0:T],
                          [r_tbk], [r_qkT])
                yield
                bkA, r_bkA = banks.next()
                mm_group(bkA[0:T, 0:T], [(qkT[:, 2 + dc, 0:T], qkT[:, dc, 0:T]) for dc in range(2)], [r_qkT], [r_bkA])
                AT, r_AT = (ATS, r_ATS) if sample else ATr.next()
                mk = maskS if sample else mask
                P.op("dve", lambda e: e.tensor_tensor(
                    out=AT[0:T, 0:T], in0=bkA[0:T, 0:T], in1=mk[0:T, 0:T], op=ALU.mult),
                    reads=[r_bkA, r_cst], writes=[r_AT])
                if sample:
                    for dc in range(2):
                        dst = bass.AP(QmT.tensor, QmT.offset + dc * 16 * 64, [list(QmT.ap[0]), [68, 16], [1, 4]])
                        P.op("dve", lambda e, dc=dc, dst=dst: e.tensor_copy(
                            out=dst, in_=qkT[:, dc, 0:64].rearrange("p (s t) -> p s t", t=4)),
                            reads=[r_qkT], writes=[r_QmT])
                out.update(T=T, sample=sample, dec=gg["dec"], kt=(kt, r_kt), qkT=(qkT, r_qkT), AT=(AT, r_AT))
                yield

            def norm_gate_g(tb, T, bkO, r_bkO):
                ss, r_ss = small.next()
                P.op("act", lambda e: e.activation(out=junk2[0:T, :], in_=bkO[0:T, :], func=AF.Square,
                                                   accum_out=ss[0:T, 0:1]),
                     reads=[r_bkO], writes=[r_junk2, r_ss])
                P.op("act", lambda e: e.activation(out=ss[0:T, 1:2], in_=ss[0:T, 0:1], func=AF.Ln,
                                                   scale=1.0 / 512, bias=eps_t[0:T, :]),
                     reads=[r_ss, r_cst], writes=[r_ss])
                P.op("act", lambda e: e.activation(out=ss[0:T, 2:3], in_=ss[0:T, 1:2], func=AF.Exp, scale=-0.5),
                     reads=[r_ss], writes=[r_ss])
                yield
                on, r_on = onr.next()
                P.op("dve", lambda e: e.scalar_tensor_tensor(
                    out=on[0:T, :], in0=bkO[0:T, :], scalar=ss[0:T, 2:3], in1=gs[0:T, tb, :], op0=ALU.mult, op1=ALU.mult),
                    reads=[r_bkO, r_ss, r_gsb[tb]], writes=[r_on])
                banks.release(r_bkO)
                yield
                transpose_to_T(on, r_on, T, oT, r_oT, tb * 128, nchunks=4, kc0=h * 4)
                yield

            def stageB(tb, c, sbfp, res):
                T = c["T"]
                dec, r_dec = c["dec"]
                kt, r_kt = c["kt"]
                qkT, r_qkT = c["qkT"]
                AT, r_AT = c["AT"]
                sbf, r_sbf = sbfp
                bkO, r_bkO = banks.next()
                banks.hold(r_bkO)
                mm_group(bkO[0:T, :], [(AT[0:T, 0:T], vv[0:T, tb, :])] +
                         [(qkT[:, dc, 0:T], sbf[:, dc, :]) for dc in range(2)],
                         [r_AT, r_vvb[tb], r_qkT, r_sbf], [r_bkO])
                yield from state_update_g(kt, r_kt, 128, vv[:, tb, :], r_vvb[tb], S, r_S[h], dec, r_dec, lambda dc: dc * 2)
                if tb < 7:
                    nsbf = Sbf.next()
                    evac_copy("act", nsbf[0], S, [r_S[h]], [nsbf[1]])
                    res["sbf"] = nsbf
                else:
                    P.op("sp", lambda e: e.dma_start(
                        out=sg_p[h].rearrange("(dc p) v -> p dc v", p=128), in_=S), reads=[r_S[h]], dma="out")
                yield from norm_gate_g(tb, T, bkO, r_bkO)

            def sample_seq_g(s, c):
                dec, r_dec = c["dec"]
                kt, r_kt = c["kt"]
                s0, r_s0 = S0r.next()
                s0b, r_s0b = S0br.next()
                Ktm, r_Ktm = Ktmr.next()
                P.op("sp", lambda e: e.dma_start(
                    out=s0, in_=sg_in[s, h].rearrange("(dc p) v -> p dc v", p=128)), writes=[r_s0], dma="st_in")
                P.op("pool", lambda e: e.tensor_scalar(
                    out=Ktm[0:64, :], in0=kt[0:64, :], scalar1=cst[0:64, K_RM + s:K_RM + s + 1], scalar2=None,
                    op0=ALU.mult), reads=[r_kt, r_cst], writes=[r_Ktm])
                yield
                evac_copy("act", s0b, s0, [r_s0], [r_s0b])
                yield

                def finter(e):
                    ins = None
                    for dc in range(2):
                        ins = e.matmul(bkOS[0:64, :], lhsT=QmT[:, dc, s, :], rhs=s0b[:, dc, :], start=False,
                                       stop=(s == 15 and dc == 1))
                    return ins
                P.op("pe", finter, reads=[r_QmT, r_s0b, r_bkOS], writes=[r_bkOS])
                for dc in range(2):
                    bk, r_bk = banks.next()
                    mm_group(bk[:, :], [(Ktm[0:64, dc * 128:(dc + 1) * 128], vS[0:64, :])],
                             [r_Ktm, r_vS], [r_bk])
                    P.op("dve", lambda e, dc=dc, bk=bk: e.scalar_tensor_tensor(
                        out=s0[:, dc, :], in0=s0[:, dc, :], scalar=dec[:, dc * 16 + s:dc * 16 + s + 1], in1=bk[:, :],
                        op0=ALU.mult, op1=ALU.add), reads=[r_bk, r_dec, r_s0], writes=[r_s0])
                yield
                P.op("sp", lambda e: e.dma_start(
                    out=sg_s[s, h].rearrange("(dc p) v -> p dc v", p=128), in_=s0), reads=[r_s0], dma="st_out")
                yield

            cS = {}
            run_gens([stageA(8, cS)])
            evac_copy("act", vS[0:64, :], vv[0:64, 8, :], [r_vvb[8]], [r_vS])
            P.op("pe", lambda e: e.matmul(bkOS[0:64, :], lhsT=ATS[0:64, 0:64], rhs=vS[0:64, :], start=True, stop=False),
                 reads=[r_ATS, r_vS], writes=[r_bkOS])
            ctx = {}
            run_gens([stageA(0, ctx)])
            cur = (sbf0, r_sbf0)
            for tb in range(8):
                gens = []
                nxt = {}
                res = {}
                if tb + 1 < 8:
                    gens.append(stageA(tb + 1, nxt))
                gens.append(stageB(tb, ctx, cur, res))
                gens.append(sample_seq_g(2 * tb, cS))
                gens.append(sample_seq_g(2 * tb + 1, cS))
                if h + 1 < 4:
                    units = [(h + 1, "qk", tb)]
                    if tb == 0:
                        units += [(h + 1, "qk", 8), (h + 1, "v", 8)]
                    else:
                        units += [(h + 1, "v", tb - 1)]
                    gens.append(proj_gen(units))
                run_gens(gens)
                if "sbf" in res:
                    cur = res["sbf"]
                ctx = nxt
            run_gens([norm_gate_g(8, 64, bkOS, r_bkOS)])
            if h + 1 < 4:
                proj_unit(h + 1, "v", 7)
                for tb in range(9):
                    proj_unit(h + 1, "g", tb)
        if DEBUG:
            P.op("sp", lambda e: e.dma_start(out=dbg["oT"], in_=oT), reads=[r_oT], dma="out")
            P.op("sp", lambda e: e.dma_start(out=dbg["hT"], in_=hT), reads=[r_hT], dma="out")
        P.barrier()
        R3.reset(); R4.reset(); R5.reset()

        cb = R3.alloc((16, NT), BF16)
        r_cb = Res()
        NU = 32 + 1024
        ubr = Ring([R4.alloc((NU,), BF16) for _ in range(2)])
        dgr = Ring([R4.alloc((31, 128), BF16) for _ in range(2)])
        sgr = Ring([R4.alloc((512,), F32) for _ in range(2)])
        sgS = R4.alloc((96,), F32)
        r_sgS = Res()
        ue = R4.alloc((16, 34), BF16)
        r_ue = Res()
        usm = R4.alloc((64,), F32)
        r_usm = Res()
        u32 = R4.alloc((32,), F32)
        r_u32 = Res()
        sctr = Ring([R4.alloc((4, 128), F32) for _ in range(2)])
        accd = R4.alloc((1024,), F32)
        r_accd = Res()
        NDT = 8
        for (sct_, r_sct_) in sctr.items:
            P.op("dve", lambda e, sct_=sct_: e.memset(sct_, 0.0), writes=[r_sct_])
        utm = R5.alloc((D,), F32)
        utp = R5.alloc((D,), F32)
        r_utm, r_utp = Res(), Res()
        sc_v = sc_in.rearrange("(sq sl) r c -> (sl r) sq c", sl=4)
        P.op("sp", lambda e: e.dma_start(out=sc_s[:, 0:26, :], in_=sc_in[:, 4:30, :]), dma="out")
        for cc in range(16):
            if cc % 2 == 0:
                slot2, r_sl = load_w([(w_in_v[:, :, C_GLUA + cc * 128:C_GLUA + (cc + 2) * 128], 0, 256),
                                      (w_in_v[:, :, C_GLUB + cc * 128:C_GLUB + (cc + 2) * 128], 256, 256)])
            slot = slot2[:, :, (cc % 2) * 128:]
            ub, r_ub = ubr.next()
            dg, r_dg = dgr.next()
            in0 = bass.AP(ident_b.tensor, ident_b.offset, [list(ident_b.ap[0]), [0, 31], [1, 128]])
            wv = wdwT[:, cc, :]
            in1 = bass.AP(wv.tensor, wv.offset, [list(wv.ap[0]), [1, 31], [0, 128]])
            P.op("dve", lambda e, dg=dg, in0=in0, in1=in1: e.tensor_tensor(out=dg, in0=in0, in1=in1, op=ALU.mult),
                 reads=[r_cst], writes=[r_dg])
            sct, r_sct = sctr.next()
            P.op("sp", lambda e, cc=cc, sct=sct: e.dma_start(out=sct[0:120, :, :], in_=sc_v[:, :, cc * 128:(cc + 1) * 128]),
                 writes=[r_sct], dma="in")
            bkh, r_bkh = banks.next()

            def fh(e, sct=sct, bkh=bkh):
                ins = None
                for sq in range(4):
                    ins = e.transpose(out=bkh[:, sq * 128:(sq + 1) * 128], in_=sct[:, sq, :], identity=ident_f)
                return ins
            P.op("pe", fh, reads=[r_sct, r_cst], writes=[r_bkh])
            for sq in range(4):
                evac_copy("dve" if sq % 2 else "act", ue[:, 4 * sq:4 * sq + 4, 0:30],
                          bkh[:, sq * 128:sq * 128 + 120].rearrange("p (a b) -> p a b", a=4), [r_bkh], [r_ue])
            for g in range(2):
                bA, r_bA = banks.next()
                bB, r_bB = banks.next()
                mm_group(bA[:, :], [(slot[:, kc, 0:128], hT[:, kc, g * 512:(g + 1) * 512]) for kc in range(16)], [r_sl, r_hT], [r_bA])
                mm_group(bB[:, :], [(slot[:, kc, 256:384], hT[:, kc, g * 512:(g + 1) * 512]) for kc in range(16)], [r_sl, r_hT], [r_bB])
                sg, r_sg = sgr.next()
                P.op("act", lambda e, sg=sg, bB=bB: e.activation(out=sg, in_=bB[:, :], func=AF.Sigmoid), reads=[r_bB], writes=[r_sg])
                P.op("dve", lambda e, g=g, ub=ub, bA=bA, sg=sg: e.tensor_tensor(
                    out=ub[:, 32 + g * 512:32 + (g + 1) * 512], in0=bA[:, :], in1=sg, op=ALU.mult),
                    reads=[r_bA, r_sg], writes=[r_ub])
                if g == 1:
                    P.op("dve", lambda e, bA=bA, sg=sg: e.tensor_tensor(out=u32, in0=bA[:, 480:512], in1=sg[:, 480:512], op=ALU.mult),
                         reads=[r_bA, r_sg], writes=[r_u32])
            bA, r_bA = banks.next()
            bB, r_bB = banks.next()

            def fsm(e, which, bank, slot=slot):
                ins = None
                c0 = 0 if which == 0 else 256
                for kc in range(16):
                    ins = e.matmul(bank[:, 0:32], lhsT=slot[:, kc, c0:c0 + 128], rhs=hp_tail[:, kc, :], start=(kc == 0), stop=(kc == 15))
                for kc in range(16):
                    ins = e.matmul(bank[:, 32:96], lhsT=slot[:, kc, c0:c0 + 128], rhs=hT[:, kc, 1024:1088], start=(kc == 0), stop=(kc == 15))
                return ins
            P.op("pe", lambda e, bA=bA: fsm(e, 0, bA), reads=[r_sl, r_hT, r_hpt], writes=[r_bA])
            P.op("pe", lambda e, bB=bB: fsm(e, 1, bB), reads=[r_sl, r_hT, r_hpt], writes=[r_bB])
            P.op("act", lambda e, bB=bB: e.activation(out=sgS, in_=bB[:, 0:96], func=AF.Sigmoid), reads=[r_bB], writes=[r_sgS])
            P.op("dve", lambda e, ub=ub, bA=bA: e.tensor_tensor(out=ub[:, 0:32], in0=bA[:, 0:32], in1=sgS[:, 0:32], op=ALU.mult),
                 reads=[r_bA, r_sgS], writes=[r_ub])
            P.op("dve", lambda e, bA=bA: e.tensor_tensor(out=usm, in0=bA[:, 32:96], in1=sgS[:, 32:96], op=ALU.mult),
                 reads=[r_bA, r_sgS], writes=[r_usm])
            P.op("dve", lambda e: e.tensor_copy(out=ue[:, :, 30:34], in_=usm.rearrange("p (s t) -> p s t", t=4)),
                 reads=[r_usm], writes=[r_ue])
            bkt, r_bkt = banks.next()

            def ft(e, bkt=bkt):
                e.transpose(out=bkt[0:32, 0:128], in_=u32, identity=ident_f)
                return e.transpose(out=bkt[0:64, 128:256], in_=usm, identity=ident_f)
            P.op("pe", ft, reads=[r_u32, r_usm, r_cst], writes=[r_bkt])
            P.op("act", lambda e, cc=cc, bkt=bkt: e.activation(out=utp[0:32, cc * 128:(cc + 1) * 128], in_=bkt[0:32, 0:128], func=AF.Copy),
                 reads=[r_bkt], writes=[r_utp])
            P.op("act", lambda e, cc=cc, bkt=bkt: e.activation(out=utm[0:64, cc * 128:(cc + 1) * 128], in_=bkt[0:64, 128:256], func=AF.Copy),
                 reads=[r_bkt], writes=[r_utm])
            P.op("dve", lambda e, cc=cc, ub=ub: e.tensor_scalar(
                out=accd, in0=ub[:, 2:2 + 1024], scalar1=wdwT[:, cc, 0:1], scalar2=vecT[:, cc:cc + 1], op0=ALU.mult, op1=ALU.add),
                reads=[r_ub, r_cst], writes=[r_accd])
            for w in range(1, NDT):
                P.op("dve", lambda e, cc=cc, w=w, ub=ub: e.scalar_tensor_tensor(
                    out=accd, in0=ub[:, 2 + w:2 + w + 1024], scalar=wdwT[:, cc, w:w + 1], in1=accd, op0=ALU.mult, op1=ALU.add),
                    reads=[r_ub, r_cst, r_accd], writes=[r_accd])
            for g in range(2):
                bkc, r_bkc = banks.next()
                mm_group(bkc[:, :], [(dg[:, w, :], ub[:, 2 + w + g * 512:2 + w + g * 512 + 512]) for w in range(NDT, 31)],
                         [r_dg, r_ub], [r_bkc])
                P.op("dve", lambda e, cc=cc, g=g, bkc=bkc: e.tensor_tensor(out=cb[:, cc, g * 512:(g + 1) * 512], in0=bkc[:, :],
                                                                        in1=accd[:, g * 512:(g + 1) * 512], op=ALU.add),
                     reads=[r_bkc, r_accd], writes=[r_cb])
            bkc, r_bkc = banks.next()
            mm_group(bkc[:, 0:64].rearrange("p (s t) -> p s t", t=4), [(dg[:, w, :], ue[:, :, w:w + 4]) for w in range(31)],
                     [r_dg, r_ue], [r_bkc])
            P.op("act", lambda e, cc=cc, bkc=bkc: e.activation(out=cb[:, cc, 1024:1088], in_=bkc[:, 0:64], func=AF.Identity,
                                                               bias=vecT[:, cc:cc + 1]), reads=[r_bkc, r_cst], writes=[r_cb])
        P.op("sp", lambda e: e.dma_start(out=sc_p, in_=utp[2:32, :]), reads=[r_utp], dma="out")
        for s in range(16):
            P.op("sp", lambda e, s=s: e.dma_start(out=sc_s[s, 26:30, :], in_=utm[4 * s:4 * s + 4, :]), reads=[r_utm], dma="out")
        if DEBUG:
            P.op("sp", lambda e: e.dma_start(out=dbg["cb0"], in_=cb), reads=[r_cb], dma="out")
            P.op("sp", lambda e: e.dma_start(out=dbg["wdwT"], in_=wdwT), reads=[r_cst], dma="out")
            P.op("sp", lambda e: e.dma_start(out=dbg["vecT"], in_=vecT), reads=[r_cst], dma="out")
        P.barrier()
        R4.reset(); R5.reset()
        mixT = R4.alloc((16, NT), BF16)
        r_mix = Res()
        sqr = Ring([R5.alloc((512,), BF16) for _ in range(2)])
        mean_sb = R5.alloc((512,), F32)
        rstd_sb = R5.alloc((512,), F32)
        m2_sb = R5.alloc((512,), F32)
        r_stat = Res()
        t1r = Ring([R5.alloc((512,), F32) for _ in range(2)])
        t2r = Ring([R5.alloc((512,), F32) for _ in range(2)])
        TG = ((0, 512), (512, 512), (1024, 64))

        def ln_gen():
            for (c0, n) in TG:
                bm, r_bm = bkOS, r_bkOS
                bq, r_bq = banks.next()
                banks.hold(r_bq)
                sqs = {}

                def mk_sq(cc, c0=c0, n=n):
                    sq, r_sq = sqr.next()
                    P.op("act", lambda e: e.activation(out=sq[:, 0:n], in_=cb[:, cc, c0:c0 + n], func=AF.Square),
                         reads=[r_cb], writes=[r_sq])
                    sqs[cc] = (sq, r_sq)
                mk_sq(0)
                for cc in range(16):
                    if cc + 1 < 16:
                        mk_sq(cc + 1)
                    sq, r_sq = sqs.pop(cc)
                    P.op("pe", lambda e, cc=cc, bm=bm, c0=c0, n=n: e.matmul(bm[:, 0:n], lhsT=onesm_b, rhs=cb[:, cc, c0:c0 + n],
                                                                            start=(cc == 0), stop=(cc == 15)),
                         reads=[r_cb, r_cst] + ([r_bm] if cc else []), writes=[r_bm])
                    P.op("pe", lambda e, cc=cc, bq=bq, sq=sq, n=n: e.matmul(bq[:, 0:n], lhsT=onesm_b, rhs=sq[:, 0:n],
                                                                            start=(cc == 0), stop=(cc == 15)),
                         reads=[r_sq, r_cst] + ([r_bq] if cc else []), writes=[r_bq])
                    if cc % 2 == 1:
                        yield
                P.op("act", lambda e, bm=bm, n=n: e.activation(out=mean_sb[:, 0:n], in_=bm[:, 0:n], func=AF.Copy), reads=[r_bm], writes=[r_stat])
                P.op("dve", lambda e, n=n: e.tensor_tensor(out=m2_sb[:, 0:n], in0=mean_sb[:, 0:n], in1=mean_sb[:, 0:n], op=ALU.mult),
                     reads=[r_stat], writes=[r_stat])
                P.op("dve", lambda e, bq=bq, n=n: e.tensor_tensor(out=m2_sb[:, 0:n], in0=bq[:, 0:n], in1=m2_sb[:, 0:n], op=ALU.subtract),
                     reads=[r_bq, r_stat], writes=[r_stat])
                banks.release(r_bq)
                P.op("act", lambda e, n=n: e.activation(out=m2_sb[:, 0:n], in_=m2_sb[:, 0:n], func=AF.Ln, bias=eps_t), reads=[r_stat, r_cst], writes=[r_stat])
                P.op("act", lambda e, n=n: e.activation(out=rstd_sb[:, 0:n], in_=m2_sb[:, 0:n], func=AF.Exp, scale=-0.5), reads=[r_stat], writes=[r_stat])
                yield
                for cc in range(16):
                    t1, r_t1 = t1r.next()
                    t2, r_t2 = t2r.next()
                    P.op("dve", lambda e, cc=cc, t1=t1, c0=c0, n=n: e.tensor_tensor(out=t1[:, 0:n], in0=cb[:, cc, c0:c0 + n], in1=mean_sb[:, 0:n], op=ALU.subtract),
                         reads=[r_cb, r_stat], writes=[r_t1])
                    P.op("dve", lambda e, t1=t1, t2=t2, n=n: e.tensor_tensor(out=t2[:, 0:n], in0=t1[:, 0:n], in1=rstd_sb[:, 0:n], op=ALU.mult),
                         reads=[r_t1, r_stat], writes=[r_t2])
                    P.op("act", lambda e, cc=cc, t2=t2, c0=c0, n=n: e.activation(out=cb[:, cc, c0:c0 + n], in_=t2[:, 0:n], func=AF.Silu,
                                                                                 scale=vecT[:, 16 + cc:17 + cc], bias=vecT[:, 32 + cc:33 + cc]),
                         reads=[r_t2, r_cst, r_cb], writes=[r_cb])
                    yield

        def g1_pass1_gen():
            for q in range(4):
                cs = q * 512
                slot, r_sl = load_w([(w_in_v[:, :, C_GA + cs:C_GA + cs + 512], 0, 512)])
                for c in range(4):
                    fc = q * 4 + c
                    for (c0, n) in TG:
                        bk, r_bk = banks.next()
                        mm_group(bk[:, 0:n], [(slot[:, kc, c * 128:(c + 1) * 128], hT[:, kc, c0:c0 + n]) for kc in range(16)], [r_sl, r_hT], [r_bk])
                        P.op("act", lambda e, fc=fc, bk=bk, c0=c0, n=n: e.activation(out=mixT[:, fc, c0:c0 + n], in_=bk[:, 0:n], func=AF.Sigmoid),
                             reads=[r_bk], writes=[r_mix])
                        yield
                slot, r_sl = load_w([(w_o_v[:, :, cs:cs + 512], 0, 512)])
                for c in range(4):
                    fc = q * 4 + c
                    for (c0, n) in TG:
                        bk, r_bk = banks.next()
                        mm_group(bk[:, 0:n], [(slot[:, kc, c * 128:(c + 1) * 128], oT[:, kc, c0:c0 + n]) for kc in range(16)], [r_sl, r_oT], [r_bk])
                        P.op("dve", lambda e, fc=fc, bk=bk, c0=c0, n=n: e.tensor_tensor(out=mixT[:, fc, c0:c0 + n], in0=bk[:, 0:n], in1=mixT[:, fc, c0:c0 + n], op=ALU.mult),
                             reads=[r_bk, r_mix], writes=[r_mix])
                        yield

        run_gens([g1_pass1_gen(), ln_gen()])
        if DEBUG:
            P.op("sp", lambda e: e.dma_start(out=dbg["cb"], in_=cb), reads=[r_cb], dma="out")
        P.barrier()
        R5.reset()

        sgbS = R5.alloc((4, NT), BF16)
        r_sgbS = Res()
        m1r = Ring([R5.alloc((512,), F32) for _ in range(2)])
        for q in range(4):
            cs = q * 512
            slot, r_sl = load_w([(w_in_v[:, :, C_GB + cs:C_GB + cs + 512], 0, 512)])
            for c in range(4):
                for (c0, n) in TG:
                    bk, r_bk = banks.next()
                    mm_group(bk[:, 0:n], [(slot[:, kc, c * 128:(c + 1) * 128], hT[:, kc, c0:c0 + n]) for kc in range(16)], [r_sl, r_hT], [r_bk])
                    P.op("act", lambda e, c=c, bk=bk, c0=c0, n=n: e.activation(out=sgbS[:, c, c0:c0 + n], in_=bk[:, 0:n], func=AF.Sigmoid),
                         reads=[r_bk], writes=[r_sgbS])
            slot, r_sl = load_w([(w_pw_v[:, :, cs:cs + 512], 0, 512)])
            for c in range(4):
                fc = q * 4 + c
                for (c0, n) in TG:
                    bk, r_bk = banks.next()
                    mm_group(bk[:, 0:n], [(slot[:, kc, c * 128:(c + 1) * 128], cb[:, kc, c0:c0 + n]) for kc in range(16)], [r_sl, r_cb], [r_bk])
                    m1, r_m1 = m1r.next()
                    P.op("dve", lambda e, c=c, m1=m1, bk=bk, c0=c0, n=n: e.tensor_tensor(out=m1[:, 0:n], in0=bk[:, 0:n], in1=sgbS[:, c, c0:c0 + n], op=ALU.mult),
                         reads=[r_bk, r_sgbS], writes=[r_m1])
                    P.op("dve", lambda e, fc=fc, m1=m1, c0=c0, n=n: e.tensor_tensor(out=mixT[:, fc, c0:c0 + n], in0=m1[:, 0:n], in1=mixT[:, fc, c0:c0 + n], op=ALU.add),
                         reads=[r_m1, r_mix], writes=[r_mix])
        if DEBUG:
            P.op("sp", lambda e: e.dma_start(out=dbg["mixT"], in_=mixT), reads=[r_mix], dma="out")
        P.barrier()
        R1.reset(); R2.reset(); R3.reset(); R5.reset()

        x1 = [R3.alloc((D,), F32)] + [R1.alloc((D,), F32) for _ in range(4)] + [R2.alloc((D,), F32) for _ in range(4)]
        x1 = x1[1:] + x1[:1]
        r_x1 = [Res() for _ in range(9)]
        for tb in range(9):
            T = 128 if tb < 8 else 64
            src = x_main[tb * 128:(tb + 1) * 128, :] if tb < 8 else x_smp
            P.op("sp", lambda e, tb=tb, T=T, src=src: e.dma_start(out=x1[tb][0:T, :], in_=src), writes=[r_x1[tb]], dma="in")
        for cg in range(4):
            slot, r_sl = load_w([(w_out_v[:, :, cg * 512:(cg + 1) * 512], 0, 512)])
            for tb in range(9):
                T = 128 if tb < 8 else 64
                bk, r_bk = banks.next()
                mm_group(bk[0:T, :], [(mixT[:, kc, tb * 128:tb * 128 + T], slot[:, kc, :]) for kc in range(16)], [r_mix, r_sl], [r_bk])
                P.op("dve", lambda e, tb=tb, T=T, cg=cg, bk=bk: e.tensor_tensor(
                    out=x1[tb][0:T, cg * 512:(cg + 1) * 512], in0=bk[0:T, :], in1=x1[tb][0:T, cg * 512:(cg + 1) * 512], op=ALU.add),
                    reads=[r_bk, r_x1[tb]], writes=[r_x1[tb]])
        if DEBUG:
            for tb in range(9):
                P.op("sp", lambda e, tb=tb: e.dma_start(out=dbg["x1"][tb], in_=x1[tb]), reads=[r_x1[tb]], dma="out")
        P.barrier()
        R4.reset()

        hfT = R4.alloc((16, NT), BF16)
        r_hfT = Res()
        gb2 = R3.alloc((D,), F32)
        r_gb2 = Res()
        P.op("sp", lambda e: e.dma_start(out=gb2, in_=g_n2[0:1, :].broadcast_to([128, D])), writes=[r_gb2], dma="in")
        hring = Ring([R3.alloc((D,), BF16) for _ in range(2)])
        junk = R3.alloc((D,), BF16)
        r_junk = Res()
        small = Ring([R3.alloc((4,), F32) for _ in range(4)])
        xsr = Ring([R3.alloc((512,), F32) for _ in range(2)])
        rms_pipeline([(128 if tb < 8 else 64, tb * 128, None, (x1[tb], r_x1[tb])) for tb in range(9)],
                     gb2, r_gb2, hfT, r_hfT, None, hring, junk, r_junk, small)
        fT = R5.alloc((8, NT), BF16)
        r_fT = Res()
        for g in range(8):
            for half in range(2):
                slot, r_sl = load_w([(w_ff1_v[:, :, g * 1024 + half * 512:g * 1024 + (half + 1) * 512], 0, 512)])
                for c in range(4):
                    j = half * 4 + c
                    for (c0, n) in ((0, 512), (512, 512), (1024, 64)):
                        bk, r_bk = banks.next()
                        mm_group(bk[:, 0:n], [(slot[:, kc, c * 128:(c + 1) * 128], hfT[:, kc, c0:c0 + n]) for kc in range(16)],
                                 [r_sl, r_hfT], [r_bk])
                        xs, r_xs = xsr.next()
                        P.op("act", lambda e, xs=xs, bk=bk, n=n: e.activation(out=xs[:, 0:n], in_=bk[:, 0:n], func=AF.Copy), reads=[r_bk], writes=[r_xs])
                        P.op("dve", lambda e, j=j, xs=xs, bk=bk, c0=c0, n=n: e.scalar_tensor_tensor(
                            out=fT[:, j, c0:c0 + n], in0=bk[:, 0:n], scalar=0.0, in1=xs[:, 0:n], op0=ALU.max, op1=ALU.mult),
                            reads=[r_bk, r_xs], writes=[r_fT])
            for half in range(2):
                slot, r_sl = wslots.next()
                slv = slot.rearrange("p a b -> p (a b)").rearrange("p (a b) -> p a b", a=4)
                P.op("pool", lambda e, g=g, half=half, slv=slv: e.dma_start(out=slv, in_=w_ff2_v[:, g * 8 + half * 4:g * 8 + half * 4 + 4, :]),
                     writes=[r_sl], dma="w")
                for tb in range(9):
                    T = 128 if tb < 8 else 64
                    for cg in range(4):
                        bk, r_bk = banks.next()
                        mm_group(bk[0:T, :], [(fT[:, half * 4 + c, tb * 128:tb * 128 + T], slv[:, c, cg * 512:(cg + 1) * 512]) for c in range(4)],
                                 [r_fT, r_sl], [r_bk])
                        P.op("dve", lambda e, tb=tb, T=T, cg=cg, bk=bk: e.tensor_tensor(
                            out=x1[tb][0:T, cg * 512:(cg + 1) * 512], in0=bk[0:T, :], in1=x1[tb][0:T, cg * 512:(cg + 1) * 512], op=ALU.add),
                            reads=[r_bk, r_x1[tb]], writes=[r_x1[tb]])
        P.fence(("act", "dve", "sp"), [r_hfT])
        R4.reset()
        gbf = R4.alloc((D,), F32)
        r_gbf = Res()
        P.op("sp", lambda e: e.dma_start(out=gbf, in_=g_fin[0:1, :].broadcast_to([128, D])), writes=[r_gbf], dma="in")
        yor = Ring([R4.alloc((D,), F32) for _ in range(2)])
        junk = R4.alloc((D,), BF16)
        r_junk = Res()
        small = Ring([R4.alloc((4,), F32) for _ in range(4)])
        for tb in range(9):
            T = 128 if tb < 8 else 64
            yo, r_yo = yor.next()
            rms_sbuf(x1[tb], r_x1[tb], T, gbf, r_gbf, yo, r_yo, junk, r_junk, small)
            dst = y_main[tb * 128:(tb + 1) * 128, :] if tb < 8 else y_smp
            P.op("sp", lambda e, T=T, yo=yo, dst=dst: e.dma_start(out=dst, in_=yo[0:T, :]), reads=[r_yo], dma="out")
        P.emit(nc)
    return nc


def _consts():
    c = np.zeros((128, CW), np.float32)
    c[:, K_ID:K_ID + 128] = np.eye(128, dtype=np.float32)
    j = np.arange(128)[:, None]
    i = np.arange(128)[None, :]
    c[:, K_TRI:K_TRI + 128] = np.where(j <= i, -1.0 / 16, 0.0)
    c[:, K_TRIU:K_TRIU + 128] = np.where(j > i, -1.0 / 16, 0.0)
    c[:, K_MASK:K_MASK + 128] = np.where(j <= i, 1.0, 0.0)
    js = np.arange(64)[:, None]
    is_ = np.arange(64)[None, :]
    same = (js // 4) == (is_ // 4)
    c[:64, K_TRIS:K_TRIS + 64] = np.where(same & (js <= is_), -1.0 / 16, 0.0)
    c[:64, K_TRIUS:K_TRIUS + 64] = np.where(same & (js > is_), -1.0 / 16, 0.0)
    c[:64, K_MASKS:K_MASKS + 64] = np.where(same & (js <= is_), 1.0, 0.0)
    rm = (np.arange(64)[:, None] // 4) == np.arange(16)[None, :]
    c[:64, K_RM:K_RM + 16] = rm
    c[:64, K_SEL:K_SEL + 16] = rm * (-1.0 / 16)
    c[:, K_LAST:K_LAST + 2] = -1.0 / 16
    c[:, K_ONESM:K_ONESM + 128] = 1.0 / 2048
    return c


_NC = None


def kernel(x_prompt, x_sample, state_gla, state_conv, w_in, w_a2, b_a, g_gla_norm, w_o_gla,
           w_dw, b_dw, g_ln, b_ln, w_pw2, w_out, g_norm1, g_norm2, w_ff1, w_ff2, g_final):
    global _NC
    f = lambda a: np.ascontiguousarray(np.asarray(a, dtype=np.float32))
    x_prompt, x_sample, state_gla, state_conv = f(x_prompt), f(x_sample), f(state_gla), f(state_conv)
    if _NC is None:
        _NC = build_program()
    nc = _NC
    shared = {
        "w_in": f(w_in)[0], "w_a2": f(w_a2)[0], "b_a": f(b_a)[0].reshape(1, 1024), "ggn": f(g_gla_norm)[0].reshape(1, 512),
        "w_o_gla": f(w_o_gla)[0], "w_dw": f(w_dw)[0],
        "vecs": np.ascontiguousarray(np.concatenate([f(b_dw)[0].reshape(16, 128), f(g_ln)[0].reshape(16, 128),
                                                     f(b_ln)[0].reshape(16, 128)], axis=0)),
        "w_pw2": f(w_pw2)[0], "w_out": f(w_out)[0], "g_n1": f(g_norm1)[0].reshape(1, D), "g_n2": f(g_norm2)[0].reshape(1, D),
        "g_fin": f(g_final).reshape(1, D), "w_ff1": f(w_ff1)[0], "w_ff2": f(w_ff2)[0], "consts": _consts(),
    }
    in_maps = []
    zeros = np.zeros((1024, D), np.float32)
    for c in range(8):
        b, half = c // 2, c % 2
        m = dict(shared)
        m["x_main"] = np.ascontiguousarray(x_prompt[b, half * 1024:(half + 1) * 1024])
        m["x_pre"] = zeros if half == 0 else np.ascontiguousarray(x_prompt[b, 0:1024])
        m["x_smp"] = np.ascontiguousarray(x_sample[16 * c:16 * c + 16].reshape(64, D))
        m["sg_in"] = np.ascontiguousarray(state_gla[0, 16 * c:16 * c + 16])
        m["sc_in"] = np.ascontiguousarray(state_conv[0, 16 * c:16 * c + 16])
        in_maps.append(m)
    res = run_bass_kernel_spmd(nc, in_maps, core_ids=list(range(8)))
    R = res.results
    if DEBUG:
        global _DBG
        _DBG = R
    y_prompt = np.stack([np.concatenate([R[2 * b]["y_main"], R[2 * b + 1]["y_main"]], axis=0) for b in range(4)])
    y_sample = np.concatenate([R[c]["y_smp"].reshape(16, 4, D) for c in range(8)], axis=0)
    sgp = np.stack([R[2 * b + 1]["sg_p"] for b in range(4)])[None]
    scp = np.stack([R[2 * b + 1]["sc_p"] for b in range(4)])[None]
    sgs = np.concatenate([R[c]["sg_s"] for c in range(8)], axis=0)[None]
    scs = np.concatenate([R[c]["sc_s"] for c in range(8)], axis=0)[None]
    return (y_prompt.astype(np.float32), y_sample.astype(np.float32), sgp.astype(np.float32),
            scp.astype(np.float32), sgs.astype(np.float32), scs.astype(np.float32))
```

```python
import contextlib
import types
import numpy as np
import concourse.bass as bass
import concourse.mybir as mybir
from concourse.bass_utils import run_bass_kernel_spmd

F32 = mybir.dt.float32
BF16 = mybir.dt.bfloat16
AF = mybir.ActivationFunctionType
ALU = mybir.AluOpType

DEBUG = False
D = 2048
DIN = 14352
NT = 1088
C_Q, C_K, C_V, C_G, C_ALR, C_GLUA, C_GLUB, C_GA, C_GB = 0, 1024, 2048, 4096, 6144, 6160, 8208, 10256, 12304
EPS = 1e-6
K_ID, K_TRI, K_TRIU, K_MASK, K_TRIS, K_TRIUS, K_MASKS, K_RM, K_SEL, K_LAST, K_ONESM = (
    0, 128, 256, 384, 512, 576, 640, 704, 720, 736, 738)
CW = 866


def _freeze(fn, depth=0):
    if not isinstance(fn, types.FunctionType) or fn.__closure__ is None or depth > 3:
        return fn
    cells = []
    for c in fn.__closure__:
        try:
            v = c.cell_contents
        except ValueError:
            cells.append(c)
            continue
        if isinstance(v, types.FunctionType):
            v = _freeze(v, depth + 1)
        cells.append(types.CellType(v))
    g = types.FunctionType(fn.__code__, fn.__globals__, fn.__name__, fn.__defaults__, tuple(cells))
    g.__kwdefaults__ = fn.__kwdefaults__
    return g


class Res:
    __slots__ = ("w", "r")

    def __init__(self):
        self.w = None
        self.r = {}


class Prog:
    ENGS = ("pe", "dve", "act", "pool", "sp")

    def __init__(self):
        self.ops = {e: [] for e in self.ENGS}
        self.cnt = {}
        self.known = {e: {} for e in self.ENGS}
        self.semkeys = []
        self.dma_i = {}

    DMA_RING = {"w": 8, "in": 8, "out": 8, "st_in": 4, "st_out": 4}

    def _sem(self, key):
        if key not in self.cnt:
            self.cnt[key] = 0
            self.semkeys.append(key)
        return key

    def op(self, eng, fn, reads=(), writes=(), dma=None):
        fn = _freeze(fn)
        waits = {}

        def need(tok):
            if tok is None:
                return
            k, v = tok
            if self.known[eng].get(k, 0) >= v:
                return
            if waits.get(k, 0) < v:
                waits[k] = v

        for r in reads:
            need(r.w)
        for w in writes:
            need(w.w)
            for k, v in w.r.items():
                need((k, v))
        if dma is None:
            key = self._sem("e_" + eng)
            inc = 1
        else:
            i = self.dma_i.get(dma, 0)
            self.dma_i[dma] = i + 1
            key = self._sem("dma_%s_%d" % (dma, i % self.DMA_RING.get(dma, 8)))
            inc = 16
            need((key, self.cnt[key]))
        for k, v in waits.items():
            self.known[eng][k] = v
        self.cnt[key] += inc
        tok = (key, self.cnt[key])
        for r in reads:
            if r.r.get(key, 0) < tok[1]:
                r.r[key] = tok[1]
        for w in writes:
            w.w = tok
            w.r = {}
        self.ops[eng].append((list(waits.items()), fn, key, inc))
        return tok

    def fence(self, engs, resources):
        for e in engs:
            waits = {}
            for r in resources:
                toks = list(r.r.items()) + ([r.w] if r.w is not None else [])
                for k, v in toks:
                    if self.known[e].get(k, 0) < v and waits.get(k, 0) < v:
                        waits[k] = v
            for k, v in waits.items():
                self.known[e][k] = v
            if waits:
                self.ops[e].append((list(waits.items()), None, None, 0))

    def barrier(self, engs=("dve", "act", "sp")):
        for e in engs:
            waits = []
            for k in self.semkeys:
                v = self.cnt[k]
                if v > 0 and self.known[e].get(k, 0) < v:
                    waits.append((k, v))
                    self.known[e][k] = v
            if waits:
                self.ops[e].append((waits, None, None, 0))

    def emit(self, nc, final_eng="sp"):
        with contextlib.ExitStack() as st:
            sems = {k: st.enter_context(nc.semaphore(k)) for k in self.semkeys}
            block = st.enter_context(nc.Block())
            engmap = {"pe": block.tensor, "dve": block.vector, "act": block.scalar,
                      "pool": block.gpsimd, "sp": block.sync}
            for e in self.ENGS:
                ops = self.ops[e]
                final = (e == final_eng)

                def body(engine, ops=ops, final=final):
                    for waits, fn, key, inc in ops:
                        for k, v in waits:
                            engine.wait_ge(sems[k], v)
                        if fn is not None:
                            fn(engine).then_inc(sems[key], inc)
                    if final:
                        for k in self.semkeys:
                            if self.cnt[k] > 0:
                                engine.wait_ge(sems[k], self.cnt[k])

                engmap[e](body)


class Arena:
    def __init__(self, nc, st, name, nbytes):
        self.t = st.enter_context(nc.sbuf_tensor(name, [128, nbytes // 2], BF16))
        self.nbytes = nbytes
        self.off = 0

    def reset(self):
        self.off = 0

    def alloc(self, shape, dt):
        n = 1
        for s in shape:
            n *= s
        nb = n * (4 if dt == F32 else 2)
        nb = (nb + 31) // 32 * 32
        assert self.off + nb <= self.nbytes, (self.off, nb, self.nbytes)
        v = self.t[:, self.off // 2:(self.off + nb) // 2]
        if dt == F32:
            v = v.bitcast(F32)
        v = v[:, 0:n]
        if len(shape) == 2:
            v = v.rearrange("p (a b) -> p a b", a=shape[0])
        elif len(shape) == 3:
            v = v.rearrange("p (a b c) -> p a b c", a=shape[0], b=shape[1])
        self.off += nb
        return v


class Ring:
    def __init__(self, items):
        self.items = [(a, Res()) for a in items]
        self.i = 0
        self.held = set()

    def next(self, exclude=None):
        for _ in range(2 * len(self.items)):
            it = self.items[self.i % len(self.items)]
            self.i += 1
            if (exclude is not None and it[1] is exclude) or (id(it[1]) in self.held):
                continue
            return it
        raise RuntimeError("ring exhausted")

    def hold(self, res):
        self.held.add(id(res))

    def release(self, res):
        self.held.discard(id(res))


def build_program():
    nc = bass.Bass("TRN2", target_bir_lowering=False)

    def din(name, shape):
        return nc.dram_tensor(name, list(shape), F32, kind="ExternalInput").ap()

    def dout(name, shape):
        return nc.dram_tensor(name, list(shape), F32, kind="ExternalOutput").ap()

    x_main = din("x_main", [1024, D])
    x_pre = din("x_pre", [1024, D])
    x_smp = din("x_smp", [64, D])
    sg_in = din("sg_in", [16, 4, 256, 512])
    sc_in = din("sc_in", [16, 30, D])
    w_in = din("w_in", [D, DIN])
    w_a2 = din("w_a2", [16, 1024])
    b_a = din("b_a", [1, 1024])
    ggn = din("ggn", [1, 512])
    w_o_gla = din("w_o_gla", [D, D])
    w_dw = din("w_dw", [31, D])
    vecs = din("vecs", [48, 128])
    w_pw2 = din("w_pw2", [D, D])
    w_out = din("w_out", [D, D])
    g_n1 = din("g_n1", [1, D])
    g_n2 = din("g_n2", [1, D])
    g_fin = din("g_fin", [1, D])
    w_ff1 = din("w_ff1", [D, 4 * D])
    w_ff2 = din("w_ff2", [4 * D, D])
    consts = din("consts", [128, CW])

    y_main = dout("y_main", [1024, D])
    y_smp = dout("y_smp", [64, D])
    sg_p = dout("sg_p", [4, 256, 512])
    sc_p = dout("sc_p", [30, D])
    sg_s = dout("sg_s", [16, 4, 256, 512])
    sc_s = dout("sc_s", [16, 30, D])

    w_in_v = w_in.rearrange("(kc p) n -> p kc n", p=128)
    w_o_v = w_o_gla.rearrange("(kc p) n -> p kc n", p=128)
    w_pw_v = w_pw2.rearrange("(kc p) n -> p kc n", p=128)
    w_out_v = w_out.rearrange("(kc p) n -> p kc n", p=128)
    w_ff1_v = w_ff1.rearrange("(kc p) n -> p kc n", p=128)
    w_ff2_v = w_ff2.rearrange("(kc p) n -> p kc n", p=128)

    dbg = {}
    if DEBUG:
        dbg["oT"] = nc.dram_tensor("dbg_oT", [128, 16, NT], BF16, kind="ExternalOutput").ap()
        dbg["cb"] = nc.dram_tensor("dbg_cb", [128, 16, NT], BF16, kind="ExternalOutput").ap()
        dbg["mixT"] = nc.dram_tensor("dbg_mixT", [128, 16, NT], BF16, kind="ExternalOutput").ap()
        dbg["hT"] = nc.dram_tensor("dbg_hT", [128, 16, NT], BF16, kind="ExternalOutput").ap()
        dbg["x1"] = nc.dram_tensor("dbg_x1", [9, 128, D], F32, kind="ExternalOutput").ap()
        dbg["O"] = nc.dram_tensor("dbg_O", [64, 512], F32, kind="ExternalOutput").ap()
        dbg["Qm"] = nc.dram_tensor("dbg_Qm", [128, 2, 16, 64], BF16, kind="ExternalOutput").ap()
        dbg["qkT"] = nc.dram_tensor("dbg_qkT", [128, 4, 128], BF16, kind="ExternalOutput").ap()
        dbg["AT"] = nc.dram_tensor("dbg_AT", [128, 128], BF16, kind="ExternalOutput").ap()
        dbg["cb0"] = nc.dram_tensor("dbg_cb0", [128, 16, NT], BF16, kind="ExternalOutput").ap()
        dbg["stat"] = nc.dram_tensor("dbg_stat", [3, 128, 512], F32, kind="ExternalOutput").ap()
        dbg["wdwT"] = nc.dram_tensor("dbg_wdwT", [128, 16, 31], F32, kind="ExternalOutput").ap()
        dbg["vecT"] = nc.dram_tensor("dbg_vecT", [128, 48], F32, kind="ExternalOutput").ap()
    P = Prog()
    st = contextlib.ExitStack()
    with st:
        AW = Arena(nc, st, "AW", 2 * 16384)
        R1 = Arena(nc, st, "R1", 34816)
        R2 = Arena(nc, st, "R2", 34816)
        R3 = Arena(nc, st, "R3", 34816)
        R4 = Arena(nc, st, "R4", 34816)
        R5 = Arena(nc, st, "R5", 18432)
        AC = Arena(nc, st, "AC", 14336)
        banks = Ring([st.enter_context(nc.psum_tensor(f"bank{i}", [128, 512], F32)) for i in range(5)])
        bkOS = st.enter_context(nc.psum_tensor("bankOS", [128, 512], F32))
        r_bkOS = Res()
        tbanks = Ring([st.enter_context(nc.psum_tensor(f"tbank{i}", [128, 1024], BF16)) for i in range(2)])
        wslots = Ring([AW.alloc((16, 512), BF16) for _ in range(2)])

        cst = AC.alloc((CW,), F32)
        ident_b = AC.alloc((128,), BF16)
        onesm_b = AC.alloc((128,), BF16)
        w_a2b = AC.alloc((1024,), F32)
        ggn_b = AC.alloc((512,), F32)
        walr = AC.alloc((16, 16), BF16)
        eps_t = AC.alloc((1,), F32)
        one_t = AC.alloc((1,), F32)
        vecT = AC.alloc((48,), F32)
        wdwT = AC.alloc((16, 31), F32)
        r_cst = Res()

        ident_f = cst[:, K_ID:K_ID + 128]

        def evac_copy(eng, out, in_, reads, writes):
            if eng == "act":
                P.op("act", lambda e: e.activation(out=out, in_=in_, func=AF.Copy), reads=reads, writes=writes)
            else:
                P.op(eng, lambda e: e.tensor_copy(out=out, in_=in_), reads=reads, writes=writes)

        def mm_group(out, pairs, reads, writes):
            def f(e):
                n = len(pairs)
                ins = None
                for i, (l, r) in enumerate(pairs):
                    ins = e.matmul(out, lhsT=l, rhs=r, start=(i == 0), stop=(i == n - 1))
                return ins
            P.op("pe", f, reads=reads, writes=writes)

        def load_w(parts):
            slot, res = wslots.next()
            for (src, c0, n) in parts:
                P.op("pool", lambda e, src=src, c0=c0, n=n: e.dma_start(out=slot[:, :, c0:c0 + n], in_=src),
                     writes=[res], dma="w")
            return slot, res

        P.op("sp", lambda e: e.dma_start(out=cst, in_=consts), writes=[r_cst], dma="in")
        P.op("sp", lambda e: e.dma_start(out=w_a2b[0:16, :], in_=w_a2), writes=[r_cst], dma="in")
        P.op("sp", lambda e: e.dma_start(out=w_a2b[16:17, :], in_=b_a), writes=[r_cst], dma="in")
        P.op("sp", lambda e: e.dma_start(out=ggn_b, in_=ggn[0:1, :].broadcast_to([128, 512])), writes=[r_cst], dma="in")
        P.op("pool", lambda e: e.dma_start(out=walr, in_=w_in_v[:, :, C_ALR:C_ALR + 16]), writes=[r_cst], dma="w")
        P.op("dve", lambda e: e.memset(eps_t, EPS), writes=[r_cst])
        P.op("dve", lambda e: e.memset(one_t, 1.0), writes=[r_cst])
        P.op("dve", lambda e: e.tensor_copy(out=ident_b, in_=ident_f), reads=[r_cst], writes=[r_cst])
        P.op("dve", lambda e: e.tensor_copy(out=onesm_b, in_=cst[:, K_ONESM:K_ONESM + 128]), reads=[r_cst], writes=[r_cst])
        tmpv = R5.alloc((128,), F32)
        r_tmpv = Res()
        P.op("dve", lambda e: e.memset(tmpv, 0.0), writes=[r_tmpv])
        P.op("sp", lambda e: e.dma_start(out=tmpv[0:48, :], in_=vecs), writes=[r_tmpv], dma="in")
        bk, rb_ = banks.next()
        P.op("pe", lambda e: e.transpose(out=bk[:, 0:128], in_=tmpv, identity=ident_f),
             reads=[r_tmpv, r_cst], writes=[rb_])
        P.op("dve", lambda e: e.tensor_copy(out=vecT, in_=bk[:, 0:48]), reads=[rb_], writes=[r_cst])
        tmpw = R1.alloc((D,), F32)
        r_tmpw = Res()
        P.op("dve", lambda e: e.memset(tmpw, 0.0), writes=[r_tmpw])
        P.op("sp", lambda e: e.dma_start(out=tmpw[0:31, :], in_=w_dw), writes=[r_tmpw], dma="in")
        for g4 in range(4):
            bk, rb_ = banks.next()

            def f_wdw(e, bk=bk, g4=g4):
                ins = None
                for j in range(4):
                    cc = g4 * 4 + j
                    ins = e.transpose(out=bk[:, j * 128:(j + 1) * 128], in_=tmpw[:, cc * 128:(cc + 1) * 128], identity=ident_f)
                return ins
            P.op("pe", f_wdw, reads=[r_tmpw, r_cst], writes=[rb_])
            P.op("dve", lambda e, bk=bk, g4=g4: e.tensor_copy(out=wdwT[:, g4 * 4:(g4 + 1) * 4, :],
                                                                in_=bk[:, 0:512].rearrange("p (a b) -> p a b", a=4)[:, :, 0:31]),
                 reads=[rb_], writes=[r_cst])
        P.barrier()
        R5.reset(); R1.reset()

        def rms_to_T(x_src, T, gb, r_gb, dstT, r_dst, col0, xring, hring, junk, r_junk, small):
            xt, r_xt = xring.next()
            hb, r_hb = hring.next()
            P.op("sp", lambda e: e.dma_start(out=xt[0:T, :], in_=x_src), writes=[r_xt], dma="in")
            rms_sbuf(xt, r_xt, T, gb, r_gb, hb, r_hb, junk, r_junk, small)
            transpose_to_T(hb, r_hb, T, dstT, r_dst, col0)

        def rms_sbuf(xt, r_xt, T, gb, r_gb, hb, r_hb, junk, r_junk, small):
            ss, r_ss = small.next()
            P.op("act", lambda e: e.activation(out=junk[0:T, :], in_=xt[0:T, :], func=AF.Square, accum_out=ss[0:T, 0:1]),
                 reads=[r_xt], writes=[r_junk, r_ss])
            P.op("act", lambda e: e.activation(out=ss[0:T, 1:2], in_=ss[0:T, 0:1], func=AF.Ln, scale=1.0 / D,
                                               bias=eps_t[0:T, :]), reads=[r_ss, r_cst], writes=[r_ss])
            P.op("act", lambda e: e.activation(out=ss[0:T, 2:3], in_=ss[0:T, 1:2], func=AF.Exp, scale=-0.5),
                 reads=[r_ss], writes=[r_ss])
            P.op("dve", lambda e: e.scalar_tensor_tensor(out=hb[0:T, :], in0=xt[0:T, :], scalar=ss[0:T, 2:3],
                                                         in1=gb[0:T, :], op0=ALU.mult, op1=ALU.mult),
                 reads=[r_xt, r_ss, r_gb], writes=[r_hb])

        def rms_pipeline(blocks, gb, r_gb, dstT, r_dst, xring, hring, junk, r_junk, small):
            st = {}

            def s1(b):
                T, col0, src, xs = blocks[b]
                if xs is None:
                    xt, r_xt = xring.next()
                    P.op("sp", lambda e: e.dma_start(out=xt[0:T, :], in_=src), writes=[r_xt], dma="in")
                else:
                    xt, r_xt = xs
                ss, r_ss = small.next()
                P.op("act", lambda e: e.activation(out=junk[0:T, :], in_=xt[0:T, :], func=AF.Square, accum_out=ss[0:T, 0:1]),
                     reads=[r_xt], writes=[r_junk, r_ss])
                P.op("act", lambda e: e.activation(out=ss[0:T, 1:2], in_=ss[0:T, 0:1], func=AF.Ln, scale=1.0 / D,
                                                   bias=eps_t[0:T, :]), reads=[r_ss, r_cst], writes=[r_ss])
                P.op("act", lambda e: e.activation(out=ss[0:T, 2:3], in_=ss[0:T, 1:2], func=AF.Exp, scale=-0.5),
                     reads=[r_ss], writes=[r_ss])
                st[b] = [xt, r_xt, ss, r_ss]

            def s2(b):
                T, col0, src, xs = blocks[b]
                xt, r_xt, ss, r_ss = st[b]
                hb, r_hb = hring.next()
                P.op("dve", lambda e: e.scalar_tensor_tensor(out=hb[0:T, :], in0=xt[0:T, :], scalar=ss[0:T, 2:3],
                                                             in1=gb[0:T, :], op0=ALU.mult, op1=ALU.mult),
                     reads=[r_xt, r_ss, r_gb], writes=[r_hb])
                st[b] = [hb, r_hb]

            def s3(b):
                T, col0, src, xs = blocks[b]
                hb, r_hb = st[b]
                transpose_to_T(hb, r_hb, T, dstT, r_dst, col0)

            n = len(blocks)
            s1(0)
            if n > 1:
                s1(1)
            s2(0)
            for b in range(n):
                if b + 2 < n:
                    s1(b + 2)
                if b + 1 < n:
                    s2(b + 1)
                s3(b)

        tcount = [0]

        def transpose_to_T(hb, r_hb, T, dstT, r_dst, col0, nchunks=16, kc0=0):
            for g in range(0, nchunks, 4):
                ng = min(4, nchunks - g)
                tb, r_tb = tbanks.next()

                def f(e, g=g, ng=ng, tb=tb):
                    ins = None
                    for j in range(ng):
                        ins = e.transpose(out=tb[:, j * 128:j * 128 + T], in_=hb[0:T, (g + j) * 128:(g + j + 1) * 128],
                                          identity=ident_b[0:T, 0:T])
                    return ins
                P.op("pe", f, reads=[r_hb, r_cst], writes=[r_tb])
                eng = "dve" if (tcount[0] % 2 == 0) else "act"
                tcount[0] += 1
                evac_copy(eng, dstT[:, kc0 + g:kc0 + g + ng, col0:col0 + T],
                          tb[:, 0:ng * 128].rearrange("p (a b) -> p a b", a=ng)[:, :, 0:T], [r_tb], [r_dst])

        def run_gens(gens):
            gens = list(gens)
            while gens:
                for g in list(gens):
                    try:
                        next(g)
                    except StopIteration:
                        gens.remove(g)

        def gla_gates_g(alr, r_alr, col0, T, h, sample, G, need_b, out, dec_tile=None):
            tri = cst[:, (K_TRIS if sample else K_TRI):][:, 0:T]
            triu = cst[:, (K_TRIUS if sample else K_TRIU):][:, 0:T]
            bk, r_bk = banks.next()
            mm_group(bk[0:T, 0:256], [(alr[0:17, col0:col0 + T], w_a2b[0:17, h * 256:(h + 1) * 256])],
                     [r_alr, r_cst], [r_bk])
            e1, r_e1 = G["e1"].next()
            P.op("act", lambda e: e.activation(out=e1[0:T, :], in_=bk[0:T, 0:256], func=AF.Exp, scale=-1.0),
                 reads=[r_bk], writes=[r_e1])
            nla, r_nla = G["nla"].next()
            P.op("act", lambda e: e.activation(out=nla[0:T, :], in_=e1[0:T, :], func=AF.Ln, bias=one_t[0:T, :]),
                 reads=[r_e1, r_cst], writes=[r_nla])
            yield
            bk2, r_bk2 = banks.next()
            mm_group(bk2[0:T, 0:256], [(triu[0:T, :], nla[0:T, :])], [r_nla, r_cst], [r_bk2])
            erb, r_erb = G["erb"].next()
            P.op("act", lambda e: e.activation(out=erb[0:T, :], in_=bk2[0:T, 0:256], func=AF.Exp),
                 reads=[r_bk2], writes=[r_erb])
            out["erb"] = (erb, r_erb)
            if need_b:
                bk3, r_bk3 = banks.next()
                mm_group(bk3[0:T, 0:256], [(tri[0:T, :], nla[0:T, :])], [r_nla, r_cst], [r_bk3])
                eb, r_eb = G["eb"].next()
                enb, r_enb = G["enb"].next()
                P.op("act", lambda e: e.activation(out=eb[0:T, :], in_=bk3[0:T, 0:256], func=AF.Exp),
                     reads=[r_bk3], writes=[r_eb])
                P.op("act", lambda e: e.activation(out=enb[0:T, :], in_=bk3[0:T, 0:256], func=AF.Exp, scale=-1.0),
                     reads=[r_bk3], writes=[r_enb])
                out["eb"] = (eb, r_eb)
                out["enb"] = (enb, r_enb)
            bk4, r_bk4 = banks.next()
            nd = 16 if sample else 2
            kcol = K_SEL if sample else K_LAST

            def fd(e):
                ins = None
                for dc in range(2):
                    ins = e.matmul(bk4[:, dc * nd:dc * nd + nd], lhsT=nla[0:T, dc * 128:(dc + 1) * 128],
                                   rhs=cst[0:T, kcol:kcol + nd], start=True, stop=True)
                return ins
            P.op("pe", fd, reads=[r_nla, r_cst], writes=[r_bk4])
            dec, r_dec = dec_tile if dec_tile is not None else G["dec"].next()
            P.op("act", lambda e: e.activation(out=dec[:, 0:2 * nd], in_=bk4[:, 0:2 * nd], func=AF.Exp),
                 reads=[r_bk4], writes=[r_dec])
            out["dec"] = (dec, r_dec)
            yield

        def state_update_g(kt, r_kt, T, v_ap, r_v, S, r_S, dec, r_dec, deccol):
            for dc in range(2):
                bk, r_bk = banks.next()
                mm_group(bk[:, :], [(kt[0:T, dc * 128:(dc + 1) * 128], v_ap)], [r_kt, r_v], [r_bk])
                c = deccol(dc)
                P.op("dve", lambda e, dc=dc, bk=bk, c=c: e.scalar_tensor_tensor(
                    out=S[:, dc, :], in0=S[:, dc, :], scalar=dec[:, c:c + 1], in1=bk[:, :],
                    op0=ALU.mult, op1=ALU.add), reads=[r_bk, r_dec, r_S], writes=[r_S])
            yield

        S_all = R5.alloc((4, 2, 512), F32)
        r_S = [Res() for _ in range(4)]
        hp_tail = AC.alloc((16, 32), BF16)
        r_hpt = Res()
        P.op("dve", lambda e: e.memset(S_all, 0.0), writes=r_S)

        hpT = R4.alloc((16, 1024), BF16)
        r_hpT = Res()
        gb1 = R1.alloc((D,), F32)
        r_gb1 = Res()
        P.op("sp", lambda e: e.dma_start(out=gb1, in_=g_n1[0:1, :].broadcast_to([128, D])), writes=[r_gb1], dma="in")
        xring = Ring([R1.alloc((D,), F32) for _ in range(2)])
        hring = Ring([R1.alloc((D,), BF16) for _ in range(2)])
        junk = R3.alloc((D,), BF16)
        r_junk = Res()
        small = Ring([R3.alloc((4,), F32) for _ in range(4)])
        rms_pipeline([(128, tb * 128, x_pre[tb * 128:(tb + 1) * 128, :], None) for tb in range(8)],
                     gb1, r_gb1, hpT, r_hpT, xring, hring, junk, r_junk, small)
        P.op("dve", lambda e: e.tensor_copy(out=hp_tail, in_=hpT[:, :, 992:1024]), reads=[r_hpT], writes=[r_hpt])

        def make_alr(actT, r_actT, ntok, alr, r_alr):
            P.op("dve", lambda e: e.memset(alr[0:17, 0:ntok], 1.0), writes=[r_alr])
            c = 0
            while c < ntok:
                n = min(512, ntok - c)
                bk, r_bk = banks.next()
                mm_group(bk[0:16, 0:n], [(walr[:, kc, :], actT[:, kc, c:c + n]) for kc in range(16)],
                         [r_cst, r_actT], [r_bk])
                P.op("dve", lambda e, bk=bk, c=c, n=n: e.tensor_copy(out=alr[0:16, c:c + n], in_=bk[0:16, 0:n]),
                     reads=[r_bk], writes=[r_alr])
                c += n

        alrp = R2.alloc((1024,), F32)
        r_alrp = Res()
        make_alr(hpT, r_hpT, 1024, alrp, r_alrp)

        P.barrier()
        R1.reset()
        vp = R1.alloc((8, 2048), BF16)
        kp = R3.alloc((8, 1024), BF16)
        r_kp = [Res() for _ in range(4)]
        r_vp = [Res() for _ in range(4)]
        G = {k: Ring([R2.alloc((256,), F32) for _ in range(2)]) for k in ("e1", "nla", "erb", "eb", "enb")}
        G["dec"] = Ring([R2.alloc((32,), F32) for _ in range(2)])
        ktr = Ring([R2.alloc((256,), BF16) for _ in range(2)])
        pslots = {}

        def pproj(h, tb):
            if tb == 0:
                pslots[h] = (load_w([(w_in_v[:, :, C_K + h * 256:C_K + (h + 1) * 256], 0, 256)]),
                             load_w([(w_in_v[:, :, C_V + h * 512:C_V + (h + 1) * 512], 0, 512)]))
            (ks, r_ks), (vs, r_vs) = pslots[h]
            bk, r_bk = banks.next()
            mm_group(bk[:, 0:256], [(hpT[:, kc, tb * 128:(tb + 1) * 128], ks[:, kc, 0:256]) for kc in range(16)],
                     [r_hpT, r_ks], [r_bk])
            evac_copy("act" if tb % 2 else "dve", kp[:, tb, h * 256:(h + 1) * 256], bk[:, 0:256], [r_bk], [r_kp[h]])
            bk, r_bk = banks.next()
            mm_group(bk[:, :], [(hpT[:, kc, tb * 128:(tb + 1) * 128], vs[:, kc, :]) for kc in range(16)],
                     [r_hpT, r_vs], [r_bk])
            evac_copy("dve" if tb % 2 else "act", vp[:, tb, h * 512:(h + 1) * 512], bk[:, :], [r_bk], [r_vp[h]])

        def pA(h, tb, out):
            gg = {}
            yield from gla_gates_g(alrp, r_alrp, tb * 128, 128, h, False, G, False, gg)
            erb, r_erb = gg["erb"]
            kt, r_kt = ktr.next()
            P.op("dve", lambda e: e.tensor_tensor(out=kt, in0=kp[:, tb, h * 256:(h + 1) * 256], in1=erb, op=ALU.mult),
                 reads=[r_kp[h], r_erb], writes=[r_kt])
            out["v"] = (kt, r_kt, gg["dec"][0], gg["dec"][1])
            yield

        def pB(h, tb, ctx):
            kt, r_kt, dec, r_dec = ctx["v"]
            yield from state_update_g(kt, r_kt, 128, vp[:, tb, h * 512:(h + 1) * 512], r_vp[h], S_all[:, h], r_S[h], dec, r_dec,
                                      lambda dc: dc * 2)

        def pP(h, tb):
            pproj(h, tb)
            yield

        for tb in range(8):
            pproj(0, tb)
        for h in range(4):
            ctx = {}
            run_gens([pA(h, 0, ctx)])
            for tb in range(8):
                gens = []
                nxt = {}
                if tb + 1 < 8:
                    gens.append(pA(h, tb + 1, nxt))
                gens.append(pB(h, tb, ctx))
                if h + 1 < 4:
                    gens.append(pP(h + 1, tb))
                run_gens(gens)
                ctx = nxt
        P.barrier()
        R1.reset(); R2.reset(); R3.reset(); R4.reset()

        hT = R1.alloc((16, NT), BF16)
        r_hT = Res()
        oT = R2.alloc((16, NT), BF16)
        r_oT = Res()
        gb1 = R4.alloc((D,), F32)
        r_gb1 = Res()
        P.op("sp", lambda e: e.dma_start(out=gb1, in_=g_n1[0:1, :].broadcast_to([128, D])), writes=[r_gb1], dma="in")
        xring = Ring([R4.alloc((D,), F32) for _ in range(2)])
        hring = Ring([R4.alloc((D,), BF16) for _ in range(2)])
        alr = R3.alloc((NT,), F32)
        r_alr = Res()
        alr_end = R3.off
        junk = R3.alloc((D,), BF16)
        r_junk = Res()
        small = Ring([R3.alloc((4,), F32) for _ in range(4)])
        rms_pipeline([(128, tb * 128, x_main[tb * 128:(tb + 1) * 128, :], None) for tb in range(8)] + [(64, 1024, x_smp, None)],
                     gb1, r_gb1, hT, r_hT, xring, hring, junk, r_junk, small)
        make_alr(hT, r_hT, NT, alr, r_alr)
        P.barrier()
        R4.reset()
        R3.off = alr_end
        qk = R3.alloc((9, 512), BF16)
        vv = R3.alloc((9, 512), BF16)
        gs = R3.alloc((9, 512), BF16)
        r_qkb = [Res() for _ in range(9)]
        r_vvb = [Res() for _ in range(9)]
        r_gsb = [Res() for _ in range(9)]
        G = {k: Ring([R4.alloc((256,), F32) for _ in range(1)]) for k in ("e1", "nla", "erb", "eb", "enb")}
        G["dec"] = Ring([R3.alloc((32,), F32) for _ in range(2)])
        qdr = Ring([R4.alloc((256,), BF16) for _ in range(2)])
        kir = Ring([R4.alloc((256,), BF16) for _ in range(2)])
        ktr = Ring([R4.alloc((256,), BF16) for _ in range(2)])
        qkTr = Ring([R4.alloc((4, 128), BF16) for _ in range(2)])
        ATr = Ring([R4.alloc((128,), BF16) for _ in range(2)])
        onr = Ring([R4.alloc((512,), BF16) for _ in range(2)])
        junk2 = R3.alloc((512,), BF16)
        r_junk2 = Res()
        small = Ring([R3.alloc((4,), F32) for _ in range(4)])
        Sbf = Ring([R4.alloc((2, 512), BF16) for _ in range(2)])
        QmT = R4.alloc((2, 16, 64), BF16)
        r_QmT = Res()
        Ktmr = Ring([R4.alloc((256,), BF16) for _ in range(2)])
        S0r = Ring([R4.alloc((2, 512), F32) for _ in range(2)])
        S0br = Ring([R4.alloc((2, 512), BF16) for _ in range(2)])
        P.op("dve", lambda e: e.memset(QmT, 0.0), writes=[r_QmT])
        ktS = R5.alloc((256,), BF16); r_ktS = Res()
        qkTS = R3.alloc((4, 128), BF16); r_qkTS = Res()
        ATS = R3.alloc((128,), BF16); r_ATS = Res()
        vS = R5.alloc((512,), BF16); r_vS = Res()
        decS = R5.alloc((32,), F32); r_decS = Res()
        pm_slots = {}

        def proj_unit(hh, kind, tb):
            if (hh, kind) not in pm_slots:
                if kind == "qk":
                    pm_slots[(hh, kind)] = load_w([(w_in_v[:, :, C_Q + hh * 256:C_Q + (hh + 1) * 256], 0, 256),
                                                   (w_in_v[:, :, C_K + hh * 256:C_K + (hh + 1) * 256], 256, 256)])
                elif kind == "v":
                    pm_slots[(hh, kind)] = load_w([(w_in_v[:, :, C_V + hh * 512:C_V + (hh + 1) * 512], 0, 512)])
                else:
                    pm_slots[(hh, kind)] = load_w([(w_in_v[:, :, C_G + hh * 512:C_G + (hh + 1) * 512], 0, 512)])
            slot, r_sl = pm_slots[(hh, kind)]
            T = 128 if tb < 8 else 64
            bk, r_bk = banks.next()
            mm_group(bk[0:T, :], [(hT[:, kc, tb * 128:tb * 128 + T], slot[:, kc, :]) for kc in range(16)],
                     [r_hT, r_sl], [r_bk])
            if kind == "qk":
                evac_copy("act" if tb % 2 else "dve", qk[0:T, tb, :], bk[0:T, :], [r_bk], [r_qkb[tb]])
            elif kind == "v":
                evac_copy("dve" if tb % 2 else "act", vv[0:T, tb, :], bk[0:T, :], [r_bk], [r_vvb[tb]])
            else:
                P.op("act", lambda e: e.activation(out=gs[0:T, tb, :], in_=bk[0:T, :], func=AF.Silu),
                     reads=[r_bk], writes=[r_gsb[tb]])
                P.op("pool", lambda e: e.tensor_tensor(out=gs[0:T, tb, :], in0=gs[0:T, tb, :],
                                                       in1=ggn_b[0:T, :], op=ALU.mult),
                     reads=[r_gsb[tb], r_cst], writes=[r_gsb[tb]])

        def proj_gen(units):
            for (hh, kind, tb) in units:
                yield
                proj_unit(hh, kind, tb)
                yield

        mask = cst[:, K_MASK:K_MASK + 128]
        maskS = cst[:, K_MASKS:K_MASKS + 64]

        for h in range(4):
            if h == 0:
                for tb in range(9):
                    proj_unit(0, "qk", tb)
                for tb in range(9):
                    proj_unit(0, "v", tb)
                for tb in range(9):
                    proj_unit(0, "g", tb)
            S = S_all[:, h]
            sbf0, r_sbf0 = Sbf.next()
            evac_copy("act", sbf0, S, [r_S[h]], [r_sbf0])

            def stageA(tb, out):
                sample = (tb == 8)
                T = 64 if sample else 128
                gg = {}
                yield from gla_gates_g(alr, r_alr, tb * 128, T, h, sample, G, True, gg,
                                       dec_tile=(decS, r_decS) if sample else None)
                eb, r_eb = gg["eb"]
                enb, r_enb = gg["enb"]
                erb, r_erb = gg["erb"]
                qd, r_qd = qdr.next()
                ki, r_ki = kir.next()
                kt, r_kt = (ktS, r_ktS) if sample else ktr.next()
                P.op("dve", lambda e: e.scalar_tensor_tensor(
                    out=qd[0:T, :], in0=qk[0:T, tb, 0:256], scalar=0.0625, in1=eb[0:T, :], op0=ALU.mult, op1=ALU.mult),
                    reads=[r_qkb[tb], r_eb], writes=[r_qd])
                P.op("dve", lambda e: e.tensor_tensor(
                    out=ki[0:T, :], in0=qk[0:T, tb, 256:512], in1=enb[0:T, :], op=ALU.mult),
                    reads=[r_qkb[tb], r_enb], writes=[r_ki])
                P.op("dve", lambda e: e.tensor_tensor(
                    out=kt[0:T, :], in0=qk[0:T, tb, 256:512], in1=erb[0:T, :], op=ALU.mult),
                    reads=[r_qkb[tb], r_erb], writes=[r_kt])
                yield
                tbk, r_tbk = tbanks.next()

                def ftr(e):
                    ins = None
                    for j in range(4):
                        src = qd if j < 2 else ki
                        dc = j % 2
                        ins = e.transpose(out=tbk[:, j * 128:j * 128 + T], in_=src[0:T, dc * 128:(dc + 1) * 128],
                                          identity=ident_b[0:T, 0:T])
                    return ins
                P.op("pe", ftr, reads=[r_qd, r_ki, r_cst], writes=[r_tbk])
                qkT, r_qkT = (qkTS, r_qkTS) if sample else qkTr.next()
                evac_copy("dve", qkT[:, :, 0:T], tbk[:, 0:512].rearrange("p (a b) -> p a b", a=4)[:, :, 0:T],
                          [r_tbk], [r_qkT])
                yield
                bkA, r_bkA = banks.next()
                mm_group(bkA[0:T, 0:T], [(qkT[:, 2 + dc, 0:T], qkT[:, dc, 0:T]) for dc in range(2)], [r_qkT], [r_bkA])
                AT, r_AT = (ATS, r_ATS) if sample else ATr.next()
                mk = maskS if sample else mask
                P.op("dve", lambda e: e.tensor_tensor(
                    out=AT[0:T, 0:T], in0=bkA[0:T, 0:T], in1=mk[0:T, 0:T], op=ALU.mult),
                    reads=[r_bkA, r_cst], writes=[r_AT])
                if sample:
                    for dc in range(2):
                        dst = bass.AP(QmT.tensor, QmT.offset + dc * 16 * 64, [list(QmT.ap[0]), [68, 16], [1, 4]])
                        P.op("dve", lambda e, dc=dc, dst=dst: e.tensor_copy(
                            out=dst, in_=qkT[:, dc, 0:64].rearrange("p (s t) -> p s t", t=4)),
                            reads=[r_qkT], writes=[r_QmT])
                out.update(T=T, sample=sample, dec=gg["dec"], kt=(kt, r_kt), qkT=(qkT, r_qkT), AT=(AT, r_AT))
                yield

            def norm_gate_g(tb, T, bkO, r_bkO):
                ss, r_ss = small.next()
                P.op("act", lambda e: e.activation(out=junk2[0:T, :], in_=bkO[0:T, :], func=AF.Square,
                                                   accum_out=ss[0:T, 0:1]),
                     reads=[r_bkO], writes=[r_junk2, r_ss])
                P.op("act", lambda e: e.activation(out=ss[0:T, 1:2], in_=ss[0:T, 0:1], func=AF.Ln,
                                                   scale=1.0 / 512, bias=eps_t[0:T, :]),
                     reads=[r_ss, r_cst], writes=[r_ss])
                P.op("act", lambda e: e.activation(out=ss[0:T, 2:3], in_=ss[0:T, 1:2], func=AF.Exp, scale=-0.5),
                     reads=[r_ss], writes=[r_ss])
                yield
                on, r_on = onr.next()
                P.op("dve", lambda e: e.scalar_tensor_tensor(
                    out=on[0:T, :], in0=bkO[0:T, :], scalar=ss[0:T, 2:3], in1=gs[0:T, tb, :], op0=ALU.mult, op1=ALU.mult),
                    reads=[r_bkO, r_ss, r_gsb[tb]], writes=[r_on])
                banks.release(r_bkO)
                yield
                transpose_to_T(on, r_on, T, oT, r_oT, tb * 128, nchunks=4, kc0=h * 4)
                yield

            def stageB(tb, c, sbfp, res):
                T = c["T"]
                dec, r_dec = c["dec"]
                kt, r_kt = c["kt"]
                qkT, r_qkT = c["qkT"]
                AT, r_AT = c["AT"]
                sbf, r_sbf = sbfp
                bkO, r_bkO = banks.next()
                banks.hold(r_bkO)
                mm_group(bkO[0:T, :], [(AT[0:T, 0:T], vv[0:T, tb, :])] +
                         [(qkT[:, dc, 0:T], sbf[:, dc, :]) for dc in range(2)],
                         [r_AT, r_vvb[tb], r_qkT, r_sbf], [r_bkO])
                yield from state_update_g(kt, r_kt, 128, vv[:, tb, :], r_vvb[tb], S, r_S[h], dec, r_dec, lambda dc: dc * 2)
                if tb < 7:
                    nsbf = Sbf.next()
                    evac_copy("act", nsbf[0], S, [r_S[h]], [nsbf[1]])
                    res["sbf"] = nsbf
                else:
                    P.op("sp", lambda e: e.dma_start(
                        out=sg_p[h].rearrange("(dc p) v -> p dc v", p=128), in_=S), reads=[r_S[h]], dma="out")
                yield from norm_gate_g(tb, T, bkO, r_bkO)

            def sample_seq_g(s, c):
                dec, r_dec = c["dec"]
                kt, r_kt = c["kt"]
                s0, r_s0 = S0r.next()
                s0b, r_s0b = S0br.next()
                Ktm, r_Ktm = Ktmr.next()
                P.op("sp", lambda e: e.dma_start(
                    out=s0, in_=sg_in[s, h].rearrange("(dc p) v -> p dc v", p=128)), writes=[r_s0], dma="st_in")
                P.op("pool", lambda e: e.tensor_scalar(
                    out=Ktm[0:64, :], in0=kt[0:64, :], scalar1=cst[0:64, K_RM + s:K_RM + s + 1], scalar2=None,
                    op0=ALU.mult), reads=[r_kt, r_cst], writes=[r_Ktm])
                yield
                evac_copy("act", s0b, s0, [r_s0], [r_s0b])
                yield

                def finter(e):
                    ins = None
                    for dc in range(2):
                        ins = e.matmul(bkOS[0:64, :], lhsT=QmT[:, dc, s, :], rhs=s0b[:, dc, :], start=False,
                                       stop=(s == 15 and dc == 1))
                    return ins
                P.op("pe", finter, reads=[r_QmT, r_s0b, r_bkOS], writes=[r_bkOS])
                for dc in range(2):
                    bk, r_bk = banks.next()
                    mm_group(bk[:, :], [(Ktm[0:64, dc * 128:(dc + 1) * 128], vS[0:64, :])],
                             [r_Ktm, r_vS], [r_bk])
                    P.op("dve", lambda e, dc=dc, bk=bk: e.scalar_tensor_tensor(
                        out=s0[:, dc, :], in0=s0[:, dc, :], scalar=dec[:, dc * 16 + s:dc * 16 + s + 1], in1=bk[:, :],
                        op0=ALU.mult, op1=ALU.add), reads=[r_bk, r_dec, r_s0], writes=[r_s0])
                yield
                P.op("sp", lambda e: e.dma_start(
                    out=sg_s[s, h].rearrange("(dc p) v -> p dc v", p=128), in_=s0), reads=[r_s0], dma="st_out")
                yield

            cS = {}
            run_gens([stageA(8, cS)])
            evac_copy("act", vS[0:64, :], vv[0:64, 8, :], [r_vvb[8]], [r_vS])
            P.op("pe", lambda e: e.matmul(bkOS[0:64, :], lhsT=ATS[0:64, 0:64], rhs=vS[0:64, :], start=True, stop=False),
                 reads=[r_ATS, r_vS], writes=[r_bkOS])
            ctx = {}
            run_gens([stageA(0, ctx)])
            cur = (sbf0, r_sbf0)
            for tb in range(8):
                gens = []
                nxt = {}
                res = {}
                if tb + 1 < 8:
                    gens.append(stageA(tb + 1, nxt))
                gens.append(stageB(tb, ctx, cur, res))
                gens.append(sample_seq_g(2 * tb, cS))
                gens.append(sample_seq_g(2 * tb + 1, cS))
                if h + 1 < 4:
                    units = [(h + 1, "qk", tb)]
                    if tb == 0:
                        units += [(h + 1, "qk", 8), (h + 1, "v", 8)]
                    else:
                        units += [(h + 1, "v", tb - 1)]
                    gens.append(proj_gen(units))
                run_gens(gens)
                if "sbf" in res:
                    cur = res["sbf"]
                ctx = nxt
            run_gens([norm_gate_g(8, 64, bkOS, r_bkOS)])
            if h + 1 < 4:
                proj_unit(h + 1, "v", 7)
                for tb in range(9):
                    proj_unit(h + 1, "g", tb)
        if DEBUG:
            P.op("sp", lambda e: e.dma_start(out=dbg["oT"], in_=oT), reads=[r_oT], dma="out")
            P.op("sp", lambda e: e.dma_start(out=dbg["hT"], in_=hT), reads=[r_hT], dma="out")
        P.barrier()
        R3.reset(); R4.reset(); R5.reset()

        cb = R3.alloc((16, NT), BF16)
        r_cb = Res()
        NU = 32 + 1024
        ubr = Ring([R4.alloc((NU,), BF16) for _ in range(2)])
        dgr = Ring([R4.alloc((31, 128), BF16) for _ in range(2)])
        sgr = Ring([R4.alloc((512,), F32) for _ in range(2)])
        sgS = R4.alloc((96,), F32)
        r_sgS = Res()
        ue = R4.alloc((16, 34), BF16)
        r_ue = Res()
        usm = R4.alloc((64,), F32)
        r_usm = Res()
        u32 = R4.alloc((32,), F32)
        r_u32 = Res()
        sctr = Ring([R4.alloc((4, 128), F32) for _ in range(2)])
        accd = R4.alloc((1024,), F32)
        r_accd = Res()
        NDT = 8
        for (sct_, r_sct_) in sctr.items:
            P.op("dve", lambda e, sct_=sct_: e.memset(sct_, 0.0), writes=[r_sct_])
        utm = R5.alloc((D,), F32)
        utp = R5.alloc((D,), F32)
        r_utm, r_utp = Res(), Res()
        sc_v = sc_in.rearrange("(sq sl) r c -> (sl r) sq c", sl=4)
        P.op("sp", lambda e: e.dma_start(out=sc_s[:, 0:26, :], in_=sc_in[:, 4:30, :]), dma="out")
        for cc in range(16):
            if cc % 2 == 0:
                slot2, r_sl = load_w([(w_in_v[:, :, C_GLUA + cc * 128:C_GLUA + (cc + 2) * 128], 0, 256),
                                      (w_in_v[:, :, C_GLUB + cc * 128:C_GLUB + (cc + 2) * 128], 256, 256)])
            slot = slot2[:, :, (cc % 2) * 128:]
            ub, r_ub = ubr.next()
            dg, r_dg = dgr.next()
            in0 = bass.AP(ident_b.tensor, ident_b.offset, [list(ident_b.ap[0]), [0, 31], [1, 128]])
            wv = wdwT[:, cc, :]
            in1 = bass.AP(wv.tensor, wv.offset, [list(wv.ap[0]), [1, 31], [0, 128]])
            P.op("dve", lambda e, dg=dg, in0=in0, in1=in1: e.tensor_tensor(out=dg, in0=in0, in1=in1, op=ALU.mult),
                 reads=[r_cst], writes=[r_dg])
            sct, r_sct = sctr.next()
            P.op("sp", lambda e, cc=cc, sct=sct: e.dma_start(out=sct[0:120, :, :], in_=sc_v[:, :, cc * 128:(cc + 1) * 128]),
                 writes=[r_sct], dma="in")
            bkh, r_bkh = banks.next()

            def fh(e, sct=sct, bkh=bkh):
                ins = None
                for sq in range(4):
                    ins = e.transpose(out=bkh[:, sq * 128:(sq + 1) * 128], in_=sct[:, sq, :], identity=ident_f)
                return ins
            P.op("pe", fh, reads=[r_sct, r_cst], writes=[r_bkh])
            for sq in range(4):
                evac_copy("dve" if sq % 2 else "act", ue[:, 4 * sq:4 * sq + 4, 0:30],
                          bkh[:, sq * 128:sq * 128 + 120].rearrange("p (a b) -> p a b", a=4), [r_bkh], [r_ue])
            for g in range(2):
                bA, r_bA = banks.next()
                bB, r_bB = banks.next()
                mm_group(bA[:, :], [(slot[:, kc, 0:128], hT[:, kc, g * 512:(g + 1) * 512]) for kc in range(16)], [r_sl, r_hT], [r_bA])
                mm_group(bB[:, :], [(slot[:, kc, 256:384], hT[:, kc, g * 512:(g + 1) * 512]) for kc in range(16)], [r_sl, r_hT], [r_bB])
                sg, r_sg = sgr.next()
                P.op("act", lambda e, sg=sg, bB=bB: e.activation(out=sg, in_=bB[:, :], func=AF.Sigmoid), reads=[r_bB], writes=[r_sg])
                P.op("dve", lambda e, g=g, ub=ub, bA=bA, sg=sg: e.tensor_tensor(
                    out=ub[:, 32 + g * 512:32 + (g + 1) * 512], in0=bA[:, :], in1=sg, op=ALU.mult),
                    reads=[r_bA, r_sg], writes=[r_ub])
                if g == 1:
                    P.op("dve", lambda e, bA=bA, sg=sg: e.tensor_tensor(out=u32, in0=bA[:, 480:512], in1=sg[:, 480:512], op=ALU.mult),
                         reads=[r_bA, r_sg], writes=[r_u32])
            bA, r_bA = banks.next()
            bB, r_bB = banks.next()

            def fsm(e, which, bank, slot=slot):
                ins = None
                c0 = 0 if which == 0 else 256
                for kc in range(16):
                    ins = e.matmul(bank[:, 0:32], lhsT=slot[:, kc, c0:c0 + 128], rhs=hp_tail[:, kc, :], start=(kc == 0), stop=(kc == 15))
                for kc in range(16):
                    ins = e.matmul(bank[:, 32:96], lhsT=slot[:, kc, c0:c0 + 128], rhs=hT[:, kc, 1024:1088], start=(kc == 0), stop=(kc == 15))
                return ins
            P.op("pe", lambda e, bA=bA: fsm(e, 0, bA), reads=[r_sl, r_hT, r_hpt], writes=[r_bA])
            P.op("pe", lambda e, bB=bB: fsm(e, 1, bB), reads=[r_sl, r_hT, r_hpt], writes=[r_bB])
            P.op("act", lambda e, bB=bB: e.activation(out=sgS, in_=bB[:, 0:96], func=AF.Sigmoid), reads=[r_bB], writes=[r_sgS])
            P.op("dve", lambda e, ub=ub, bA=bA: e.tensor_tensor(out=ub[:, 0:32], in0=bA[:, 0:32], in1=sgS[:, 0:32], op=ALU.mult),
                 reads=[r_bA, r_sgS], writes=[r_ub])
            P.op("dve", lambda e, bA=bA: e.tensor_tensor(out=usm, in0=bA[:, 32:96], in1=sgS[:, 32:96], op=ALU.mult),
                 reads=[r_bA, r_sgS], writes=[r_usm])
            P.op("dve", lambda e: e.tensor_copy(out=ue[:, :, 30:34], in_=usm.rearrange("p (s t) -> p s t", t=4)),
                 reads=[r_usm], writes=[r_ue])
            bkt, r_bkt = banks.next()

            def ft(e, bkt=bkt):
                e.transpose(out=bkt[0:32, 0:128], in_=u32, identity=ident_f)
                return e.transpose(out=bkt[0:64, 128:256], in_=usm, identity=ident_f)
            P.op("pe", ft, reads=[r_u32, r_usm, r_cst], writes=[r_bkt])
            P.op("act", lambda e, cc=cc, bkt=bkt: e.activation(out=utp[0:32, cc * 128:(cc + 1) * 128], in_=bkt[0:32, 0:128], func=AF.Copy),
                 reads=[r_bkt], writes=[r_utp])
            P.op("act", lambda e, cc=cc, bkt=bkt: e.activation(out=utm[0:64, cc * 128:(cc + 1) * 128], in_=bkt[0:64, 128:256], func=AF.Copy),
                 reads=[r_bkt], writes=[r_utm])
            P.op("dve", lambda e, cc=cc, ub=ub: e.tensor_scalar(
                out=accd, in0=ub[:, 2:2 + 1024], scalar1=wdwT[:, cc, 0:1], scalar2=vecT[:, cc:cc + 1], op0=ALU.mult, op1=ALU.add),
                reads=[r_ub, r_cst], writes=[r_accd])
            for w in range(1, NDT):
                P.op("dve", lambda e, cc=cc, w=w, ub=ub: e.scalar_tensor_tensor(
                    out=accd, in0=ub[:, 2 + w:2 + w + 1024], scalar=wdwT[:, cc, w:w + 1], in1=accd, op0=ALU.mult, op1=ALU.add),
                    reads=[r_ub, r_cst, r_accd], writes=[r_accd])
            for g in range(2):
                bkc, r_bkc = banks.next()
                mm_group(bkc[:, :], [(dg[:, w, :], ub[:, 2 + w + g * 512:2 + w + g * 512 + 512]) for w in range(NDT, 31)],
                         [r_dg, r_ub], [r_bkc])
                P.op("dve", lambda e, cc=cc, g=g, bkc=bkc: e.tensor_tensor(out=cb[:, cc, g * 512:(g + 1) * 512], in0=bkc[:, :],
                                                                        in1=accd[:, g * 512:(g + 1) * 512], op=ALU.add),
                     reads=[r_bkc, r_accd], writes=[r_cb])
            bkc, r_bkc = banks.next()
            mm_group(bkc[:, 0:64].rearrange("p (s t) -> p s t", t=4), [(dg[:, w, :], ue[:, :, w:w + 4]) for w in range(31)],
                     [r_dg, r_ue], [r_bkc])
            P.op("act", lambda e, cc=cc, bkc=bkc: e.activation(out=cb[:, cc, 1024:1088], in_=bkc[:, 0:64], func=AF.Identity,
                                                               bias=vecT[:, cc:cc + 1]), reads=[r_bkc, r_cst], writes=[r_cb])
        P.op("sp", lambda e: e.dma_start(out=sc_p, in_=utp[2:32, :]), reads=[r_utp], dma="out")
        for s in range(16):
            P.op("sp", lambda e, s=s: e.dma_start(out=sc_s[s, 26:30, :], in_=utm[4 * s:4 * s + 4, :]), reads=[r_utm], dma="out")
        if DEBUG:
            P.op("sp", lambda e: e.dma_start(out=dbg["cb0"], in_=cb), reads=[r_cb], dma="out")
            P.op("sp", lambda e: e.dma_start(out=dbg["wdwT"], in_=wdwT), reads=[r_cst], dma="out")
            P.op("sp", lambda e: e.dma_start(out=dbg["vecT"], in_=vecT), reads=[r_cst], dma="out")
        P.barrier()
        R4.reset(); R5.reset()
        mixT = R4.alloc((16, NT), BF16)
        r_mix = Res()
        sqr = Ring([R5.alloc((512,), BF16) for _ in range(2)])
        mean_sb = R5.alloc((512,), F32)
        rstd_sb = R5.alloc((512,), F32)
        m2_sb = R5.alloc((512,), F32)
        r_stat = Res()
        t1r = Ring([R5.alloc((512,), F32) for _ in range(2)])
        t2r = Ring([R5.alloc((512,), F32) for _ in range(2)])
        TG = ((0, 512), (512, 512), (1024, 64))

        def ln_gen():
            for (c0, n) in TG:
                bm, r_bm = bkOS, r_bkOS
                bq, r_bq = banks.next()
                banks.hold(r_bq)
                sqs = {}

                def mk_sq(cc, c0=c0, n=n):
                    sq, r_sq = sqr.next()
                    P.op("act", lambda e: e.activation(out=sq[:, 0:n], in_=cb[:, cc, c0:c0 + n], func=AF.Square),
                         reads=[r_cb], writes=[r_sq])
                    sqs[cc] = (sq, r_sq)
                mk_sq(0)
                for cc in range(16):
                    if cc + 1 < 16:
                        mk_sq(cc + 1)
                    sq, r_sq = sqs.pop(cc)
                    P.op("pe", lambda e, cc=cc, bm=bm, c0=c0, n=n: e.matmul(bm[:, 0:n], lhsT=onesm_b, rhs=cb[:, cc, c0:c0 + n],
                                                                            start=(cc == 0), stop=(cc == 15)),
                         reads=[r_cb, r_cst] + ([r_bm] if cc else []), writes=[r_bm])
                    P.op("pe", lambda e, cc=cc, bq=bq, sq=sq, n=n: e.matmul(bq[:, 0:n], lhsT=onesm_b, rhs=sq[:, 0:n],
                                                                            start=(cc == 0), stop=(cc == 15)),
                         reads=[r_sq, r_cst] + ([r_bq] if cc else []), writes=[r_bq])
                    if cc % 2 == 1:
                        yield
                P.op("act", lambda e, bm=bm, n=n: e.activation(out=mean_sb[:, 0:n], in_=bm[:, 0:n], func=AF.Copy), reads=[r_bm], writes=[r_stat])
                P.op("dve", lambda e, n=n: e.tensor_tensor(out=m2_sb[:, 0:n], in0=mean_sb[:, 0:n], in1=mean_sb[:, 0:n], op=ALU.mult),
                     reads=[r_stat], writes=[r_stat])
                P.op("dve", lambda e, bq=bq, n=n: e.tensor_tensor(out=m2_sb[:, 0:n], in0=bq[:, 0:n], in1=m2_sb[:, 0:n], op=ALU.subtract),
                     reads=[r_bq, r_stat], writes=[r_stat])
                banks.release(r_bq)
                P.op("act", lambda e, n=n: e.activation(out=m2_sb[:, 0:n], in_=m2_sb[:, 0:n], func=AF.Ln, bias=eps_t), reads=[r_stat, r_cst], writes=[r_stat])
                P.op("act", lambda e, n=n: e.activation(out=rstd_sb[:, 0:n], in_=m2_sb[:, 0:n], func=AF.Exp, scale=-0.5), reads=[r_stat], writes=[r_stat])
                yield
                for cc in range(16):
                    t1, r_t1 = t1r.next()
                    t2, r_t2 = t2r.next()
                    P.op("dve", lambda e, cc=cc, t1=t1, c0=c0, n=n: e.tensor_tensor(out=t1[:, 0:n], in0=cb[:, cc, c0:c0 + n], in1=mean_sb[:, 0:n], op=ALU.subtract),
                         reads=[r_cb, r_stat], writes=[r_t1])
                    P.op("dve", lambda e, t1=t1, t2=t2, n=n: e.tensor_tensor(out=t2[:, 0:n], in0=t1[:, 0:n], in1=rstd_sb[:, 0:n], op=ALU.mult),
                         reads=[r_t1, r_stat], writes=[r_t2])
                    P.op("act", lambda e, cc=cc, t2=t2, c0=c0, n=n: e.activation(out=cb[:, cc, c0:c0 + n], in_=t2[:, 0:n], func=AF.Silu,
                                                                                 scale=vecT[:, 16 + cc:17 + cc], bias=vecT[:, 32 + cc:33 + cc]),
                         reads=[r_t2, r_cst, r_cb], writes=[r_cb])
                    yield

        def g1_pass1_gen():
            for q in range(4):
                cs = q * 512
                slot, r_sl = load_w([(w_in_v[:, :, C_GA + cs:C_GA + cs + 512], 0, 512)])
                for c in range(4):
                    fc = q * 4 + c
                    for (c0, n) in TG:
                        bk, r_bk = banks.next()
                        mm_group(bk[:, 0:n], [(slot[:, kc, c * 128:(c + 1) * 128], hT[:, kc, c0:c0 + n]) for kc in range(16)], [r_sl, r_hT], [r_bk])
                        P.op("act", lambda e, fc=fc, bk=bk, c0=c0, n=n: e.activation(out=mixT[:, fc, c0:c0 + n], in_=bk[:, 0:n], func=AF.Sigmoid),
                             reads=[r_bk], writes=[r_mix])
                        yield
                slot, r_sl = load_w([(w_o_v[:, :, cs:cs + 512], 0, 512)])
                for c in range(4):
                    fc = q * 4 + c
                    for (c0, n) in TG:
                        bk, r_bk = banks.next()
                        mm_group(bk[:, 0:n], [(slot[:, kc, c * 128:(c + 1) * 128], oT[:, kc, c0:c0 + n]) for kc in range(16)], [r_sl, r_oT], [r_bk])
                        P.op("dve", lambda e, fc=fc, bk=bk, c0=c0, n=n: e.tensor_tensor(out=mixT[:, fc, c0:c0 + n], in0=bk[:, 0:n], in1=mixT[:, fc, c0:c0 + n], op=ALU.mult),
                             reads=[r_bk, r_mix], writes=[r_mix])
                        yield

        run_gens([ln_gen(), g1_pass1_gen()])
        if DEBUG:
            P.op("sp", lambda e: e.dma_start(out=dbg["cb"], in_=cb), reads=[r_cb], dma="out")
        P.barrier()
        R5.reset()

        sgbS = R5.alloc((4, NT), BF16)
        r_sgbS = Res()
        m1r = Ring([R5.alloc((512,), F32) for _ in range(2)])
        for q in range(4):
            cs = q * 512
            slot, r_sl = load_w([(w_in_v[:, :, C_GB + cs:C_GB + cs + 512], 0, 512)])
            for c in range(4):
                for (c0, n) in TG:
                    bk, r_bk = banks.next()
                    mm_group(bk[:, 0:n], [(slot[:, kc, c * 128:(c + 1) * 128], hT[:, kc, c0:c0 + n]) for kc in range(16)], [r_sl, r_hT], [r_bk])
                    P.op("act", lambda e, c=c, bk=bk, c0=c0, n=n: e.activation(out=sgbS[:, c, c0:c0 + n], in_=bk[:, 0:n], func=AF.Sigmoid),
                         reads=[r_bk], writes=[r_sgbS])
            slot, r_sl = load_w([(w_pw_v[:, :, cs:cs + 512], 0, 512)])
            for c in range(4):
                fc = q * 4 + c
                for (c0, n) in TG:
                    bk, r_bk = banks.next()
                    mm_group(bk[:, 0:n], [(slot[:, kc, c * 128:(c + 1) * 128], cb[:, kc, c0:c0 + n]) for kc in range(16)], [r_sl, r_cb], [r_bk])
                    m1, r_m1 = m1r.next()
                    P.op("dve", lambda e, c=c, m1=m1, bk=bk, c0=c0, n=n: e.tensor_tensor(out=m1[:, 0:n], in0=bk[:, 0:n], in1=sgbS[:, c, c0:c0 + n], op=ALU.mult),
                         reads=[r_bk, r_sgbS], writes=[r_m1])
                    P.op("dve", lambda e, fc=fc, m1=m1, c0=c0, n=n: e.tensor_tensor(out=mixT[:, fc, c0:c0 + n], in0=m1[:, 0:n], in1=mixT[:, fc, c0:c0 + n], op=ALU.add),
                         reads=[r_m1, r_mix], writes=[r_mix])
        if DEBUG:
            P.op("sp", lambda e: e.dma_start(out=dbg["mixT"], in_=mixT), reads=[r_mix], dma="out")
        P.barrier()
        R1.reset(); R2.reset(); R3.reset(); R5.reset()

        x1 = [R3.alloc((D,), F32)] + [R1.alloc((D,), F32) for _ in range(4)] + [R2.alloc((D,), F32) for _ in range(4)]
        x1 = x1[1:] + x1[:1]
        r_x1 = [Res() for _ in range(9)]
        for tb in range(9):
            T = 128 if tb < 8 else 64
            src = x_main[tb * 128:(tb + 1) * 128, :] if tb < 8 else x_smp
            P.op("sp", lambda e, tb=tb, T=T, src=src: e.dma_start(out=x1[tb][0:T, :], in_=src), writes=[r_x1[tb]], dma="in")
        for cg in range(4):
            slot, r_sl = load_w([(w_out_v[:, :, cg * 512:(cg + 1) * 512], 0, 512)])
            for tb in range(9):
                T = 128 if tb < 8 else 64
                bk, r_bk = banks.next()
                mm_group(bk[0:T, :], [(mixT[:, kc, tb * 128:tb * 128 + T], slot[:, kc, :]) for kc in range(16)], [r_mix, r_sl], [r_bk])
                P.op("dve", lambda e, tb=tb, T=T, cg=cg, bk=bk: e.tensor_tensor(
                    out=x1[tb][0:T, cg * 512:(cg + 1) * 512], in0=bk[0:T, :], in1=x1[tb][0:T, cg * 512:(cg + 1) * 512], op=ALU.add),
                    reads=[r_bk, r_x1[tb]], writes=[r_x1[tb]])
        if DEBUG:
            for tb in range(9):
                P.op("sp", lambda e, tb=tb: e.dma_start(out=dbg["x1"][tb], in_=x1[tb]), reads=[r_x1[tb]], dma="out")
        P.barrier()
        R4.reset()

        hfT = R4.alloc((16, NT), BF16)
        r_hfT = Res()
        gb2 = R3.alloc((D,), F32)
        r_gb2 = Res()
        P.op("sp", lambda e: e.dma_start(out=gb2, in_=g_n2[0:1, :].broadcast_to([128, D])), writes=[r_gb2], dma="in")
        hring = Ring([R3.alloc((D,), BF16) for _ in range(2)])
        junk = R3.alloc((D,), BF16)
        r_junk = Res()
        small = Ring([R3.alloc((4,), F32) for _ in range(4)])
        xsr = Ring([R3.alloc((512,), F32) for _ in range(2)])
        rms_pipeline([(128 if tb < 8 else 64, tb * 128, None, (x1[tb], r_x1[tb])) for tb in range(9)],
                     gb2, r_gb2, hfT, r_hfT, None, hring, junk, r_junk, small)
        fT = R5.alloc((8, NT), BF16)
        r_fT = Res()
        for g in range(8):
            for half in range(2):
                slot, r_sl = load_w([(w_ff1_v[:, :, g * 1024 + half * 512:g * 1024 + (half + 1) * 512], 0, 512)])
                for c in range(4):
                    j = half * 4 + c
                    for (c0, n) in ((0, 512), (512, 512), (1024, 64)):
                        bk, r_bk = banks.next()
                        mm_group(bk[:, 0:n], [(slot[:, kc, c * 128:(c + 1) * 128], hfT[:, kc, c0:c0 + n]) for kc in range(16)],
                                 [r_sl, r_hfT], [r_bk])
                        xs, r_xs = xsr.next()
                        P.op("act", lambda e, xs=xs, bk=bk, n=n: e.activation(out=xs[:, 0:n], in_=bk[:, 0:n], func=AF.Copy), reads=[r_bk], writes=[r_xs])
                        P.op("dve", lambda e, j=j, xs=xs, bk=bk, c0=c0, n=n: e.scalar_tensor_tensor(
                            out=fT[:, j, c0:c0 + n], in0=bk[:, 0:n], scalar=0.0, in1=xs[:, 0:n], op0=ALU.max, op1=ALU.mult),
                            reads=[r_bk, r_xs], writes=[r_fT])
            for half in range(2):
                slot, r_sl = wslots.next()
                slv = slot.rearrange("p a b -> p (a b)").rearrange("p (a b) -> p a b", a=4)
                P.op("pool", lambda e, g=g, half=half, slv=slv: e.dma_start(out=slv, in_=w_ff2_v[:, g * 8 + half * 4:g * 8 + half * 4 + 4, :]),
                     writes=[r_sl], dma="w")
                for tb in range(9):
                    T = 128 if tb < 8 else 64
                    for cg in range(4):
                        bk, r_bk = banks.next()
                        mm_group(bk[0:T, :], [(fT[:, half * 4 + c, tb * 128:tb * 128 + T], slv[:, c, cg * 512:(cg + 1) * 512]) for c in range(4)],
                                 [r_fT, r_sl], [r_bk])
                        P.op("dve", lambda e, tb=tb, T=T, cg=cg, bk=bk: e.tensor_tensor(
                            out=x1[tb][0:T, cg * 512:(cg + 1) * 512], in0=bk[0:T, :], in1=x1[tb][0:T, cg * 512:(cg + 1) * 512], op=ALU.add),
                            reads=[r_bk, r_x1[tb]], writes=[r_x1[tb]])
        P.fence(("act", "dve", "sp"), [r_hfT])
        R4.reset()
        gbf = R4.alloc((D,), F32)
        r_gbf = Res()
        P.op("sp", lambda e: e.dma_start(out=gbf, in_=g_fin[0:1, :].broadcast_to([128, D])), writes=[r_gbf], dma="in")
        yor = Ring([R4.alloc((D,), F32) for _ in range(2)])
        junk = R4.alloc((D,), BF16)
        r_junk = Res()
        small = Ring([R4.alloc((4,), F32) for _ in range(4)])
        for tb in range(9):
            T = 128 if tb < 8 else 64
            yo, r_yo = yor.next()
            rms_sbuf(x1[tb], r_x1[tb], T, gbf, r_gbf, yo, r_yo, junk, r_junk, small)
            dst = y_main[tb * 128:(tb + 1) * 128, :] if tb < 8 else y_smp
            P.op("sp", lambda e, T=T, yo=yo, dst=dst: e.dma_start(out=dst, in_=yo[0:T, :]), reads=[r_yo], dma="out")
        P.emit(nc)
    return nc


def _consts():
    c = np.zeros((128, CW), np.float32)
    c[:, K_ID:K_ID + 128] = np.eye(128, dtype=np.float32)
    j = np.arange(128)[:, None]
    i = np.arange(128)[None, :]
    c[:, K_TRI:K_TRI + 128] = np.where(j <= i, -1.0 / 16, 0.0)
    c[:, K_TRIU:K_TRIU + 128] = np.where(j > i, -1.0 / 16, 0.0)
    c[:, K_MASK:K_MASK + 128] = np.where(j <= i, 1.0, 0.0)
    js = np.arange(64)[:, None]
    is_ = np.arange(64)[None, :]
    same = (js // 4) == (is_ // 4)
    c[:64, K_TRIS:K_TRIS + 64] = np.where(same & (js <= is_), -1.0 / 16, 0.0)
    c[:64, K_TRIUS:K_TRIUS + 64] = np.where(same & (js > is_), -1.0 / 16, 0.0)
    c[:64, K_MASKS:K_MASKS + 64] = np.where(same & (js <= is_), 1.0, 0.0)
    rm = (np.arange(64)[:, None] // 4) == np.arange(16)[None, :]
    c[:64, K_RM:K_RM + 16] = rm
    c[:64, K_SEL:K_SEL + 16] = rm * (-1.0 / 16)
    c[:, K_LAST:K_LAST + 2] = -1.0 / 16
    c[:, K_ONESM:K_ONESM + 128] = 1.0 / 2048
    return c


_NC = None


def kernel(x_prompt, x_sample, state_gla, state_conv, w_in, w_a2, b_a, g_gla_norm, w_o_gla,
           w_dw, b_dw, g_ln, b_ln, w_pw2, w_out, g_norm1, g_norm2, w_ff1, w_ff2, g_final):
    global _NC
    f = lambda a: np.ascontiguousarray(np.asarray(a, dtype=np.float32))
    x_prompt, x_sample, state_gla, state_conv = f(x_prompt), f(x_sample), f(state_gla), f(state_conv)
    if _NC is None:
        _NC = build_program()
    nc = _NC
    shared = {
        "w_in": f(w_in)[0], "w_a2": f(w_a2)[0], "b_a": f(b_a)[0].reshape(1, 1024), "ggn": f(g_gla_norm)[0].reshape(1, 512),
        "w_o_gla": f(w_o_gla)[0], "w_dw": f(w_dw)[0],
        "vecs": np.ascontiguousarray(np.concatenate([f(b_dw)[0].reshape(16, 128), f(g_ln)[0].reshape(16, 128),
                                                     f(b_ln)[0].reshape(16, 128)], axis=0)),
        "w_pw2": f(w_pw2)[0], "w_out": f(w_out)[0], "g_n1": f(g_norm1)[0].reshape(1, D), "g_n2": f(g_norm2)[0].reshape(1, D),
        "g_fin": f(g_final).reshape(1, D), "w_ff1": f(w_ff1)[0], "w_ff2": f(w_ff2)[0], "consts": _consts(),
    }
    in_maps = []
    zeros = np.zeros((1024, D), np.float32)
    for c in range(8):
        b, half = c // 2, c % 2
        m = dict(shared)
        m["x_main"] = np.ascontiguousarray(x_prompt[b, half * 1024:(half + 1) * 1024])
        m["x_pre"] = zeros if half == 0 else np.ascontiguousarray(x_prompt[b, 0:1024])
        m["x_smp"] = np.ascontiguousarray(x_sample[16 * c:16 * c + 16].reshape(64, D))
        m["sg_in"] = np.ascontiguousarray(state_gla[0, 16 * c:16 * c + 16])
        m["sc_in"] = np.ascontiguousarray(state_conv[0, 16 * c:16 * c + 16])
        in_maps.append(m)
    res = run_bass_kernel_spmd(nc, in_maps, core_ids=list(range(8)))
    R = res.results
    if DEBUG:
        global _DBG
        _DBG = R
    y_prompt = np.stack([np.concatenate([R[2 * b]["y_main"], R[2 * b + 1]["y_main"]], axis=0) for b in range(4)])
    y_sample = np.concatenate([R[c]["y_smp"].reshape(16, 4, D) for c in range(8)], axis=0)
    sgp = np.stack([R[2 * b + 1]["sg_p"] for b in range(4)])[None]
    scp = np.stack([R[2 * b + 1]["sc_p"] for b in range(4)])[None]
    sgs = np.concatenate([R[c]["sg_s"] for c in range(8)], axis=0)[None]
    scs = np.concatenate([R[c]["sc_s"] for c in range(8)], axis=0)[None]
    return (y_prompt.astype(np.float32), y_sample.astype(np.float32), sgp.astype(np.float32),
            scp.astype(np.float32), sgs.astype(np.float32), scs.astype(np.float32))
```
